# Optimizing a Trainium2 kernel written in Bass

```python
import jax, jax.numpy as jnp
from jax import lax
import numpy as np

D_MODEL = 1024
BATCH = 4
SEQ = 8192
DEPTH = 2

EPS = 1e-6
LN_EPS = 1e-5
CONV_WIDTH = 512
CONV_K = 31
N_HEADS = 8
QK_NOPE_DIM = 64
QK_ROPE_DIM = 32
V_HEAD_DIM = 64
Q_LORA_RANK = 256
KV_LORA_RANK = 128
ROPE_BASE = 10000.0
Q_BLOCK = 128
ATTN_WIDTH = N_HEADS * V_HEAD_DIM
IN_EVEN = 2 * CONV_WIDTH + Q_LORA_RANK + KV_LORA_RANK + QK_ROPE_DIM
SPLIT_EVEN = [CONV_WIDTH, 2 * CONV_WIDTH, 2 * CONV_WIDTH + Q_LORA_RANK,
              2 * CONV_WIDTH + Q_LORA_RANK + KV_LORA_RANK]
MIX_EVEN = CONV_WIDTH + ATTN_WIDTH
SSM_WIDTH = 512
SSM_GROUP = 16
SSM_GROUPS = SSM_WIDTH // SSM_GROUP
SSM_STATE = 64
DT_MIN = 0.001
DT_MAX = 0.1
D_FF = 2816
FFN_K = 3

kernel_name = "hybrid_conv_mla_s5_convffn"


def rms_norm(x, g):
    xf = x.astype(jnp.float32)
    y = xf * lax.rsqrt(jnp.mean(xf * xf, axis=-1, keepdims=True) + EPS)
    return (y * g.astype(jnp.float32)).astype(x.dtype)


def layer_norm(x, g, b):
    xf = x.astype(jnp.float32)
    mu = jnp.mean(xf, axis=-1, keepdims=True)
    var = jnp.mean(jnp.square(xf - mu), axis=-1, keepdims=True)
    y = (xf - mu) * lax.rsqrt(var + LN_EPS)
    return (y * g.astype(jnp.float32) + b.astype(jnp.float32)).astype(x.dtype)


def causal_dwconv(u, w, b):
    k, c = w.shape
    y = lax.conv_general_dilated(u, w[:, None, :].astype(u.dtype), window_strides=(1,),
                                 padding=[(k - 1, 0)],
                                 dimension_numbers=('NWC', 'WIO', 'NWC'),
                                 feature_group_count=c)
    return y + b.astype(u.dtype)


def rope(t, pos):
    half = QK_ROPE_DIM // 2
    inv = ROPE_BASE ** (-jnp.arange(half, dtype=jnp.float32) / half)
    ang = pos.astype(jnp.float32)[:, None] * inv[None, :]
    cos, sin = jnp.cos(ang)[:, None, :], jnp.sin(ang)[:, None, :]
    tf = t.astype(jnp.float32)
    t1, t2 = tf[..., :half], tf[..., half:]
    return jnp.concatenate([t1 * cos - t2 * sin, t1 * sin + t2 * cos], axis=-1).astype(t.dtype)


def mla_attention(q_nope, q_rope, k_nope, k_rope, v):
    b, s, h, _ = q_nope.shape
    nb = s // Q_BLOCK
    scale = (QK_NOPE_DIM + QK_ROPE_DIM) ** -0.5
    k_pos = jnp.arange(s)

    def blocks(t):
        return jnp.moveaxis(t.reshape((b, nb, Q_BLOCK) + t.shape[2:]), 1, 0)

    def one_block(args):
        qn, qr, i = args
        scores = (jnp.einsum('bqhd,bkhd->bhqk', qn, k_nope).astype(jnp.float32)
                  + jnp.einsum('bqhr,bkr->bhqk', qr, k_rope).astype(jnp.float32)) * scale
        q_pos = i * Q_BLOCK + jnp.arange(Q_BLOCK)
        mask = k_pos[None, :] <= q_pos[:, None]
        scores = jnp.where(mask[None, None], scores, -jnp.inf)
        p = jax.nn.softmax(scores, axis=-1).astype(v.dtype)
        return jnp.einsum('bhqk,bkhd->bqhd', p, v)

    out = lax.map(one_block, (blocks(q_nope), blocks(q_rope), jnp.arange(nb)))
    return jnp.moveaxis(out, 0, 1).reshape(b, s, h * V_HEAD_DIM)


def conv_attn_mixer(xn, w_in, conv_w, conv_b, conv_ln_g, conv_ln_b,
                    q_norm, kv_norm, w_uq, w_ukv, w_out):
    b, s, _ = xn.shape
    h = xn @ w_in
    glu_a, glu_g, c_q, c_kv, k_r = jnp.split(h, SPLIT_EVEN, axis=-1)
    u = glu_a * jax.nn.sigmoid(glu_g)
    u = causal_dwconv(u, conv_w, conv_b)
    u = jax.nn.silu(layer_norm(u, conv_ln_g, conv_ln_b))
    pos = jnp.arange(s)
    q = (rms_norm(c_q, q_norm) @ w_uq).reshape(b, s, N_HEADS, QK_NOPE_DIM + QK_ROPE_DIM)
    q_nope, q_rope = q[..., :QK_NOPE_DIM], rope(q[..., QK_NOPE_DIM:], pos)
    kv = (rms_norm(c_kv, kv_norm) @ w_ukv).reshape(b, s, N_HEADS, QK_NOPE_DIM + V_HEAD_DIM)
    k_nope, v = kv[..., :QK_NOPE_DIM], kv[..., QK_NOPE_DIM:]
    k_rope = rope(k_r[:, :, None, :], pos)[:, :, 0, :]
    attn = mla_attention(q_nope, q_rope, k_nope, k_rope, v)
    return jnp.concatenate([u, attn], axis=-1) @ w_out


def s5_mixer(xn, w_in, log_dt, a_re, a_im, b_re, b_im, c_re, c_im, d_skip, w_glu, b_glu):
    f32 = jnp.float32
    bsz, s, _ = xn.shape
    u = (xn @ w_in).astype(f32).reshape(bsz, s, SSM_GROUPS, SSM_GROUP)
    dt = jnp.exp(log_dt.astype(f32))[:, None]
    ar, ai = a_re.astype(f32), a_im.astype(f32)
    mag = jnp.exp(ar * dt)
    lb_re, lb_im = mag * jnp.cos(ai * dt), mag * jnp.sin(ai * dt)
    den = ar * ar + ai * ai
    nr, ni = lb_re - 1.0, lb_im
    f_re = (nr * ar + ni * ai) / den
    f_im = (ni * ar - nr * ai) / den
    br, bi = b_re.astype(f32), b_im.astype(f32)
    bb_re = f_re[..., None] * br - f_im[..., None] * bi
    bb_im = f_re[..., None] * bi + f_im[..., None] * br
    bu_re = jnp.einsum('bsgc,gpc->bsgp', u, bb_re)
    bu_im = jnp.einsum('bsgc,gpc->bsgp', u, bb_im)
    lam_re = jnp.broadcast_to(lb_re, bu_re.shape)
    lam_im = jnp.broadcast_to(lb_im, bu_re.shape)

    def combine(e1, e2):
        a1r, a1i, b1r, b1i = e1
        a2r, a2i, b2r, b2i = e2
        return (a2r * a1r - a2i * a1i, a2r * a1i + a2i * a1r,
                a2r * b1r - a2i * b1i + b2r, a2r * b1i + a2i * b1r + b2i)

    _, _, x_re, x_im = lax.associative_scan(combine, (lam_re, lam_im, bu_re, bu_im), axis=1)
    y = (jnp.einsum('gcp,bsgp->bsgc', c_re.astype(f32), x_re)
         - jnp.einsum('gcp,bsgp->bsgc', c_im.astype(f32), x_im)
         + d_skip.astype(f32).reshape(SSM_GROUPS, SSM_GROUP) * u)
    y = jax.nn.gelu(y.reshape(bsz, s, SSM_WIDTH)).astype(xn.dtype)
    z = y @ w_glu + b_glu
    return z[..., :D_MODEL] * jax.nn.sigmoid(z[..., D_MODEL:])


def conv_ffn(xn, w_up, conv_w, conv_b, w_down):
    h = causal_dwconv(xn @ w_up, conv_w, conv_b)
    return (jax.nn.silu(h[..., :D_FF]) * h[..., D_FF:]) @ w_down


def setup_inputs(seed: int = 0) -> dict:
    key = jax.random.key(seed)
    keys = iter(jax.random.split(key, 64))
    f32 = jnp.float32

    def nrm(shape, scale):
        return jax.random.normal(next(keys), shape, f32) * scale

    def gain(n):
        return 1.0 + nrm((n,), 0.01)

    d = D_MODEL
    inp = {}
    inp["x"] = nrm((BATCH, SEQ, d), 1.0)
    inp["l0_mix_norm"] = gain(d)
    inp["l0_w_in"] = nrm((d, IN_EVEN), d ** -0.5)
    inp["l0_conv_w"] = nrm((CONV_K, CONV_WIDTH), CONV_K ** -0.5)
    inp["l0_conv_b"] = nrm((CONV_WIDTH,), 0.01)
    inp["l0_conv_ln_g"] = gain(CONV_WIDTH)
    inp["l0_conv_ln_b"] = nrm((CONV_WIDTH,), 0.01)
    inp["l0_q_norm"] = gain(Q_LORA_RANK)
    inp["l0_kv_norm"] = gain(KV_LORA_RANK)
    inp["l0_w_uq"] = nrm((Q_LORA_RANK, N_HEADS * (QK_NOPE_DIM + QK_ROPE_DIM)), Q_LORA_RANK ** -0.5)
    inp["l0_w_ukv"] = nrm((KV_LORA_RANK, N_HEADS * (QK_NOPE_DIM + V_HEAD_DIM)), KV_LORA_RANK ** -0.5)
    inp["l0_w_out"] = nrm((MIX_EVEN, d), MIX_EVEN ** -0.5)
    inp["l0_ffn_norm"] = gain(d)
    inp["l0_w_up"] = nrm((d, 2 * D_FF), d ** -0.5)
    inp["l0_ffn_conv_w"] = nrm((FFN_K, 2 * D_FF), FFN_K ** -0.5)
    inp["l0_ffn_conv_b"] = nrm((2 * D_FF,), 0.01)
    inp["l0_w_down"] = nrm((D_FF, d), D_FF ** -0.5)
    inp["l1_mix_norm"] = gain(d)
    inp["l1_w_in"] = nrm((d, SSM_WIDTH), d ** -0.5)
    inp["l1_log_dt"] = jax.random.uniform(next(keys), (SSM_GROUPS,), f32,
                                          float(np.log(DT_MIN)), float(np.log(DT_MAX)))
    inp["l1_a_re"] = -0.5 + nrm((SSM_GROUPS, SSM_STATE), 0.01)
    inp["l1_a_im"] = (jnp.pi * jnp.arange(SSM_STATE, dtype=f32))[None, :] + nrm((SSM_GROUPS, SSM_STATE), 0.01)
    inp["l1_b_re"] = nrm((SSM_GROUPS, SSM_STATE, SSM_GROUP), (2 * SSM_GROUP) ** -0.5)
    inp["l1_b_im"] = nrm((SSM_GROUPS, SSM_STATE, SSM_GROUP), (2 * SSM_GROUP) ** -0.5)
    inp["l1_c_re"] = nrm((SSM_GROUPS, SSM_GROUP, SSM_STATE), SSM_STATE ** -0.5)
    inp["l1_c_im"] = nrm((SSM_GROUPS, SSM_GROUP, SSM_STATE), SSM_STATE ** -0.5)
    inp["l1_d"] = nrm((SSM_WIDTH,), 1.0)
    inp["l1_w_glu"] = nrm((SSM_WIDTH, 2 * d), SSM_WIDTH ** -0.5)
    inp["l1_b_glu"] = nrm((2 * d,), 0.01)
    inp["l1_ffn_norm"] = gain(d)
    inp["l1_w_up"] = nrm((d, 2 * D_FF), d ** -0.5)
    inp["l1_ffn_conv_w"] = nrm((FFN_K, 2 * D_FF), FFN_K ** -0.5)
    inp["l1_ffn_conv_b"] = nrm((2 * D_FF,), 0.01)
    inp["l1_w_down"] = nrm((D_FF, d), D_FF ** -0.5)
    inp["final_norm"] = gain(d)
    return inp


def reference(x,
              l0_mix_norm, l0_w_in, l0_conv_w, l0_conv_b, l0_conv_ln_g, l0_conv_ln_b,
              l0_q_norm, l0_kv_norm, l0_w_uq, l0_w_ukv, l0_w_out,
              l0_ffn_norm, l0_w_up, l0_ffn_conv_w, l0_ffn_conv_b, l0_w_down,
              l1_mix_norm, l1_w_in, l1_log_dt, l1_a_re, l1_a_im, l1_b_re, l1_b_im,
              l1_c_re, l1_c_im, l1_d, l1_w_glu, l1_b_glu,
              l1_ffn_norm, l1_w_up, l1_ffn_conv_w, l1_ffn_conv_b, l1_w_down,
              final_norm):
    layers = [
        (conv_attn_mixer,
         (l0_w_in, l0_conv_w, l0_conv_b, l0_conv_ln_g, l0_conv_ln_b,
          l0_q_norm, l0_kv_norm, l0_w_uq, l0_w_ukv, l0_w_out),
         l0_mix_norm, (l0_w_up, l0_ffn_conv_w, l0_ffn_conv_b, l0_w_down), l0_ffn_norm),
        (s5_mixer,
         (l1_w_in, l1_log_dt, l1_a_re, l1_a_im, l1_b_re, l1_b_im,
          l1_c_re, l1_c_im, l1_d, l1_w_glu, l1_b_glu),
         l1_mix_norm, (l1_w_up, l1_ffn_conv_w, l1_ffn_conv_b, l1_w_down), l1_ffn_norm),
    ]
    for i in range(DEPTH):
        mixer, mix_params, mix_g, ffn_params, ffn_g = layers[i]
        x = x + mixer(rms_norm(x, mix_g), *mix_params)
        x = x + conv_ffn(rms_norm(x, ffn_g), *ffn_params)
    return rms_norm(x, final_norm)
```

```python
import contextlib
import math
import numpy as np
import concourse.bass as bass
import concourse.mybir as mybir
from concourse.bass_utils import run_bass_kernel_spmd

F32 = mybir.dt.float32
BF16 = mybir.dt.bfloat16
I32 = mybir.dt.int32
AF = mybir.ActivationFunctionType
ALU = mybir.AluOpType

D = 1024
DFF = 2816
EPS = 1e-6
LN_EPS = 1e-5


class Sched:
    ENG = ('pe', 'act', 'dve', 'pool', 'sp')

    def __init__(self, nc):
        self.nc = nc
        self.ops = {e: [] for e in self.ENG}
        self.lastw = {}
        self.readers = {}
        self.known = {e: {} for e in self.ENG}
        self.latest = {}

    def op(self, eng, fn, reads=(), writes=(), dma=None):
        waits = {}

        def need(sig):
            if sig is None:
                return
            k, v = sig
            if eng == 'pe' and k == ('eng', 'pe'):
                return
            if waits.get(k, 0) < v:
                waits[k] = v
        for b in reads:
            need(self.lastw.get(b))
        for b in writes:
            need(self.lastw.get(b))
            for k, v in self.readers.get(b, {}).items():
                need((k, v))
        kn = self.known[eng]
        w = []
        for k, v in waits.items():
            if kn.get(k, 0) >= v:
                continue
            kn[k] = v
            w.append((k, v))
        if dma is not None:
            key = ('dma', dma)
            inc = 16
        else:
            key = ('eng', eng)
            inc = 1
        val = self.latest.get(key, 0) + inc
        self.latest[key] = val
        sig = (key, val)
        self.ops[eng].append((w, fn, key, inc))
        for b in reads:
            r = self.readers.setdefault(b, {})
            if r.get(key, 0) < val:
                r[key] = val
        for b in writes:
            self.lastw[b] = sig
            self.readers[b] = {}
        return sig

    def barrier(self, skip_casts=False):
        for e in self.ENG:
            kn = self.known[e]
            w = []
            for k, v in self.latest.items():
                if skip_casts and k[0] == 'dma' and isinstance(k[1], tuple) and k[1][0] == 'cast':
                    continue
                if kn.get(k, 0) >= v:
                    continue
                kn[k] = v
                w.append((k, v))
            if w:
                self.ops[e].append((w, None, None, 0))

    def emit(self):
        nc = self.nc
        with contextlib.ExitStack() as st:
            sems = {}
            for i, k in enumerate(sorted(self.latest.keys(), key=str)):
                sems[k] = st.enter_context(nc.semaphore("s%d" % i))
            block = st.enter_context(nc.Block())

            def run(name, e):
                for (w, fn, key, inc) in self.ops[name]:
                    for (k, v) in w:
                        e.wait_ge(sems[k], v)
                    if fn is not None:
                        fn(e).then_inc(sems[key], inc)

            @block.tensor
            def _(e):
                run('pe', e)

            @block.scalar
            def _(e):
                run('act', e)

            @block.vector
            def _(e):
                run('dve', e)

            @block.gpsimd
            def _(e):
                run('pool', e)

            @block.sync
            def _(e):
                run('sp', e)


PARAM_SHAPES = [
    ("l0_mix_norm", [1024]), ("l0_w_in", [1024, 1440]), ("l0_conv_w", [31, 512]), ("l0_conv_b", [512]),
    ("l0_conv_ln_g", [512]), ("l0_conv_ln_b", [512]), ("l0_q_norm", [256]), ("l0_kv_norm", [128]),
    ("l0_w_uq", [256, 768]), ("l0_w_ukv", [128, 1024]), ("l0_w_out", [1024, 1024]), ("l0_ffn_norm", [1024]),
    ("l0_w_up", [1024, 5632]), ("l0_ffn_conv_w", [3, 5632]), ("l0_ffn_conv_b", [5632]), ("l0_w_down", [2816, 1024]),
    ("l1_mix_norm", [1024]), ("l1_w_in", [1024, 512]), ("l1_log_dt", [32]), ("l1_a_re", [32, 64]),
    ("l1_a_im", [32, 64]), ("l1_b_re", [32, 64, 16]), ("l1_b_im", [32, 64, 16]), ("l1_c_re", [32, 16, 64]),
    ("l1_c_im", [32, 16, 64]), ("l1_d", [512]), ("l1_w_glu", [512, 2048]), ("l1_b_glu", [2048]),
    ("l1_ffn_norm", [1024]), ("l1_w_up", [1024, 5632]), ("l1_ffn_conv_w", [3, 5632]), ("l1_ffn_conv_b", [5632]),
    ("l1_w_down", [2816, 1024]), ("final_norm", [1024]),
]


def build(S, dbg=False):
    NT = S // 512
    NB = S // 128
    nc = bass.Bass("TRN2", target_bir_lowering=False)
    I = {}
    I["x"] = nc.dram_tensor("x", [S, D], F32, kind="ExternalInput").ap()
    for name, shp in PARAM_SHAPES:
        I[name] = nc.dram_tensor(name, shp, F32, kind="ExternalInput").ap()
    I["w_uq_sw"] = nc.dram_tensor("w_uq_sw", [256, 768], F32, kind="ExternalInput").ap()
    I["rq_c"] = nc.dram_tensor("rq_c", [32, S], F32, kind="ExternalInput").ap()
    I["rq_s"] = nc.dram_tensor("rq_s", [32, S], F32, kind="ExternalInput").ap()
    I["rk_c"] = nc.dram_tensor("rk_c", [S, 16], F32, kind="ExternalInput").ap()
    I["rk_s"] = nc.dram_tensor("rk_s", [S, 16], F32, kind="ExternalInput").ap()
    out = nc.dram_tensor("out", [S, D], F32, kind="ExternalOutput").ap()

    skind = dict(kind="ExternalOutput") if dbg else {}

    def scr(name, shp, dt, **kw):
        return nc.dram_tensor(name, shp, dt, **kw).ap()
    WB = {}
    for name in ["l0_w_in", "l0_w_uq", "w_uq_sw", "l0_w_ukv", "l0_w_out", "l1_w_in", "l1_w_glu"]:
        WB[name] = scr("bf_" + name, list(I[name].shape), BF16)
    for L_ in range(2):
        WB["l%d_w_up" % L_] = scr("bf_l%d_w_up" % L_, [11, 128, 4096], BF16)
        WB["l%d_w_down" % L_] = scr("bf_l%d_w_down" % L_, [4, 128, 22 * 256], BF16)
    hp_scr = scr("hp_scr", [S, D], F32, **skind)
    h1_scr = scr("h1_scr", [S, D], F32, **skind)
    h1m_scr = scr("h1m_scr", [S, D], F32, **skind)
    att_scr = scr("att_scr", [512, S], BF16)

    P = Sched(nc)

    def MM(o, lhsT, rhs, start, stop, reads, writes):
        P.op('pe', lambda e: e.matmul(o, lhsT=lhsT, rhs=rhs, start=start, stop=stop), reads, writes)

    def TR(o, in_, ident, reads, writes):
        P.op('pe', lambda e: e.transpose(o, in_, ident), reads, writes)

    def ACT(o, in_, func, reads, writes, scale=None, bias=None, accum=None):
        kw = {}
        if scale is not None:
            kw['scale'] = scale
        if bias is not None:
            kw['bias'] = bias
        if accum is not None:
            kw['accum_out'] = accum
        P.op('act', lambda e: e.activation(out=o, in_=in_, func=func, **kw), reads, writes)

    def TS(eng, o, in0, s1, s2, op0, op1, reads, writes):
        if op1 is None:
            P.op(eng, lambda e: e.tensor_scalar(out=o, in0=in0, scalar1=s1, scalar2=None, op0=op0), reads, writes)
        else:
            P.op(eng, lambda e: e.tensor_scalar(out=o, in0=in0, scalar1=s1, scalar2=s2, op0=op0, op1=op1), reads, writes)

    def STT(o, in0, scalar, in1, op0, op1, reads, writes):
        P.op('dve', lambda e: e.scalar_tensor_tensor(out=o, in0=in0, scalar=scalar, in1=in1, op0=op0, op1=op1), reads, writes)

    def TT(eng, o, in0, in1, op, reads, writes):
        P.op(eng, lambda e: e.tensor_tensor(out=o, in0=in0, in1=in1, op=op), reads, writes)

    def CP(eng, o, in_, reads, writes):
        P.op(eng, lambda e: e.tensor_copy(out=o, in_=in_), reads, writes)

    def RCP(o, in_, reads, writes):
        P.op('dve', lambda e: e.reciprocal(out=o, in_=in_), reads, writes)

    def MS(eng, ap, val, writes):
        P.op(eng, lambda e: e.memset(ap, val), (), writes)

    def DMA(o, in_, reads, writes, key, eng='sp', slow=False, maxlast=None):
        kw = {}
        if slow:
            kw['allow_slow_non_contiguous'] = True
        if maxlast is not None:
            kw['max_dma_last_dim'] = maxlast
        P.op(eng, lambda e: e.dma_start(out=o, in_=in_, **kw), reads, writes, dma=key)

    def SCAN(o, d0, d1, init, reads, writes):
        P.op('dve', lambda e: e.tensor_tensor_scan(out=o, data0=d0, data1=d1, initial=init, op0=ALU.mult, op1=ALU.add), reads, writes)

    with contextlib.ExitStack() as top:
        uid = [0]

        def sbt(st, name, shape, dt):
            uid[0] += 1
            return st.enter_context(nc.sbuf_tensor("%s_%d" % (name, uid[0]), shape, dt))

        def pst(st, name, shape, dt):
            uid[0] += 1
            return st.enter_context(nc.psum_tensor("%s_%d" % (name, uid[0]), shape, dt))

        ident = sbt(top, "ident", [128, 128], BF16)
        identf = sbt(top, "identf", [128, 128], F32)
        colv = sbt(top, "colv", [128, 640], F32)
        gfin = sbt(top, "gfin", [128, 1024], F32)
        eps_c = sbt(top, "eps_c", [128, 1], F32)
        MS('pool', eps_c[:], 0.0, ['eps_c'])

        MS('pool', identf[:], 1.0, ['identf'])
        P.op('pool', lambda e: e.affine_select(out=identf[:], in_=identf[:], pattern=[[-1, 128]], compare_op=ALU.is_equal,
                                               fill=0.0, base=0, channel_multiplier=1), ['identf'], ['identf'])
        CP('dve', ident[:], identf[:], ['identf'], ['ident'])
        def cast_plain(name):
            rows = I[name].shape[0]
            for r0 in range(0, rows, 128):
                DMA(WB[name][r0:r0 + 128, :], I[name][r0:r0 + 128, :], (), [('wb', name)], ('cast', name), eng='pool', maxlast=4096)

        def cast_ffn(L_):
            nu = "l%d_w_up" % L_
            nd = "l%d_w_down" % L_
            for m in range(11):
                for half in range(2):
                    c0 = half * DFF + m * 256
                    src = I[nu][:, c0:c0 + 256].rearrange("(k p) c -> p k c", p=128)
                    dst = WB[nu][m].rearrange("p (k c) -> p k c", c=512)[:, :, half * 256:(half + 1) * 256]
                    DMA(dst, src, (), [('wb', nu)], ('cast', nu), eng='pool', maxlast=4096)
            for q in range(4):
                src = I[nd][:, q * 256:(q + 1) * 256].rearrange("(k p) c -> p k c", p=128)
                dst = WB[nd][q].rearrange("p (k c) -> p k c", c=256)
                DMA(dst, src, (), [('wb', nd)], ('cast', nd), eng='pool', maxlast=4096)
        for name in ["l0_w_in", "l0_w_out", "l0_w_uq", "w_uq_sw", "l0_w_ukv"]:
            cast_plain(name)
        cast_ffn(0)
        cast_plain("l1_w_in")
        cast_plain("l1_w_glu")
        cast_ffn(1)
        DMA(gfin[:], I["final_norm"].partition_broadcast(128), (), ['gfin'], 'gfin')

        VEC = [("l0_mix_norm", 8), ("l0_ffn_norm", 8), ("l1_mix_norm", 8), ("l1_ffn_norm", 8),
               ("l0_conv_b", 4), ("l0_conv_ln_g", 4), ("l0_conv_ln_b", 4), ("l0_q_norm", 2), ("l0_kv_norm", 1),
               ("l1_d", 4), ("l1_a_re", 16), ("l1_a_im", 16),
               ("l0_conv_w", 124), ("l0_ffn_conv_w", 132), ("l0_ffn_conv_b", 44),
               ("l1_ffn_conv_w", 132), ("l1_ffn_conv_b", 44)]
        COL = {}
        r = 0
        for name, n in VEC:
            COL[name] = r
            r += n
        NST = (r + 127) // 128
        assert NST * 128 <= 640
        with contextlib.ExitStack() as st:
            stage = sbt(st, "stage", [128, NST, 128], F32)
            pstg = pst(st, "pstg", [128, 512], F32)
            MS('dve', stage[:], 0.0, ['stage'])
            stoks = []
            for name, n in VEC:
                flat = I[name]
                if len(flat.shape) == 2:
                    flat = flat.rearrange("a b -> (a b)")
                src = flat.rearrange("(r c) -> r c", c=128)
                done = 0
                while done < n:
                    r0 = COL[name] + done
                    cnt = min(n - done, 128 - (r0 % 128))
                    tok = ('stg', name, done)
                    stoks.append(tok)
                    DMA(stage[r0 % 128:r0 % 128 + cnt, r0 // 128, :], src[done:done + cnt, :], ['stage'], [tok], 'stage')
                    done += cnt
            for t in range(NST):
                TR(pstg[:, 0:128], stage[:, t, :], identf[:], stoks + ['stage', 'identf'], ['pstg'])
                CP('dve', colv[:, t * 128:(t + 1) * 128], pstg[:, 0:128], ['pstg'], ['colv'])
        P.barrier(skip_casts=True)

        def col(name, i=0, n=1):
            c = COL[name] + i
            return colv[:, c:c + n]

        def norm_stages(h, hkey, gname, xs, xnT, xnkey, tp, ssq, rstd, par=0):
            xsk = ('xs', par)
            sk = ('ssq', par)
            rk = ('rstd', par)

            def s_sq():
                for b in range(4):
                    ACT(xs[:, b, :], h[:, b, :], AF.Square, [hkey], [xsk, sk], accum=ssq[:, b:b + 1])

            def s_rstd():
                TS('dve', rstd[:], ssq[:], 1.0 / D, EPS, ALU.mult, ALU.add, [sk], [rk])
                ACT(rstd[:], rstd[:], AF.Sqrt, [rk], [rk])
                RCP(rstd[:], rstd[:], [rk], [rk])

            def s_scale():
                for b in range(4):
                    TS('dve', xs[:, b, :], h[:, b, :], rstd[:, b:b + 1], None, ALU.mult, None, [hkey, rk], [xsk])

            def s_tr(k0, k1):
                for kc in range(k0, k1):
                    t = tp[kc % 2]
                    tk = ('tp', kc % 2)
                    for b in range(4):
                        TR(t[:, b * 128:(b + 1) * 128], xs[:, b, kc * 128:(kc + 1) * 128], ident[:], [xsk, 'ident'], [tk])
                    ACT(xnT[:, kc, :], t[:, 0:512], AF.Copy, [tk, 'colv'], [(xnkey, kc)], scale=col(gname, kc))
            return [s_sq, s_rstd, s_scale] + [(lambda k=k: s_tr(k, k + 1)) for k in range(8)]

        def norm_T(st_name, h, hkey, gname, xs, xnT, xnkey, tp, ssq, rstd, par=0):
            for f in norm_stages(h, hkey, gname, xs, xnT, xnkey, tp, ssq, rstd, par):
                f()

        with contextlib.ExitStack() as st:
            xt = [sbt(st, "xt%d" % i, [128, 4, 1024], F32) for i in range(3)]
            xsb = [sbt(st, "xs%d" % i, [128, 4, 1024], BF16) for i in range(2)]
            xnTb = [sbt(st, "xnT%d" % i, [128, 8, 512], BF16) for i in range(2)]
            ssqb = [sbt(st, "ssq%d" % i, [128, 4], F32) for i in range(2)]
            rstdb = [sbt(st, "rstd%d" % i, [128, 4], F32) for i in range(2)]
            w_in = sbt(st, "w_in", [128, 8, 1024], BF16)
            w_o = sbt(st, "w_o", [128, 4, 1024], BF16)
            diag = sbt(st, "diag", [128, 124, 128], BF16)
            ubuf = sbt(st, "ubuf", [128, 4, 544], BF16)
            sig = [sbt(st, "sig%d" % i, [128, 512], BF16) for i in range(2)]
            ycv = sbt(st, "ycv", [128, 4, 512], F32)
            ysq = sbt(st, "ysq", [128, 4, 512], F32)
            mean = sbt(st, "mean", [128, 512], F32)
            var = sbt(st, "var", [128, 512], F32)
            dcv = [sbt(st, "dcv%d" % i, [128, 512], F32) for i in range(2)]
            cvT = sbt(st, "cvT", [128, 4, 512], BF16)
            onesf = sbt(st, "onesf", [128, 128], F32)
            tpb = [pst(st, "tpa%d" % i, [128, 1024], BF16) for i in range(2)]
            bk = [pst(st, "bk%d" % i, [128, 512], F32) for i in range(6)]

            MS('dve', onesf[:], 1.0 / 512.0, ['onesf'])
            MS('dve', ubuf[:], 0.0, [('ub', c_) for c_ in range(4)])
            DMA(w_in[:], WB["l0_w_in"][:, 0:1024].rearrange("(k p) c -> p k c", p=128), [('wb', "l0_w_in")], ['w_in'], 'w_in')
            DMA(w_o[:], WB["l0_w_out"][0:512, :].rearrange("(k p) c -> p k c", p=128), [('wb', "l0_w_out")], ['w_o'], 'w_o')
            for kk in range(31):
                for c in range(4):
                    TS('dve', diag[:, kk * 4 + c, :], identf[:], col("l0_conv_w", kk * 4 + c), None, ALU.mult, None,
                       ['identf', 'colv'], ['diag'])

            def load_x(i):
                DMA(xt[i % 3][:], I["x"][i * 512:(i + 1) * 512, :].rearrange("(b p) d -> p b d", p=128), (), [('xt', i % 3)], ('xt', i % 3))
            def pre_a1(i):
                return norm_stages(xt[i % 3], ('xt', i % 3), "l0_mix_norm", xsb[i % 2], xnTb[i % 2], ('xnT', i % 2), tpb,
                                   ssqb[i % 2], rstdb[i % 2], par=i % 2)
            load_x(0)
            if NT > 1:
                load_x(1)
            for f in pre_a1(0):
                f()
            for i in range(NT):
                if i + 2 < NT:
                    load_x(i + 2)
                h = xt[i % 3]
                hk = ('xt', i % 3)
                xnT = xnTb[i % 2]
                xr = [(('xnT', i % 2), kc) for kc in range(8)]
                stg = pre_a1(i + 1) if i + 1 < NT else []
                stg_it = iter(stg)

                def unit(n=1):
                    for _ in range(n):
                        f = next(stg_it, None)
                        if f is not None:
                            f()
                if i > 0:
                    for c in range(4):
                        CP('dve', ubuf[:, c, 0:30], ubuf[:, c, 512:542], [('ub', c)], [('ub', c)])
                for c in range(4):
                    pa, pak = bk[(c % 2) * 2], ('bk', (c % 2) * 2)
                    pg, pgk = bk[(c % 2) * 2 + 1], ('bk', (c % 2) * 2 + 1)
                    for kc in range(8):
                        MM(pa[:], w_in[:, kc, c * 128:(c + 1) * 128], xnT[:, kc, :], kc == 0, kc == 7, ['w_in', xr[kc]], [pak])
                    for kc in range(8):
                        MM(pg[:], w_in[:, kc, 512 + c * 128:512 + (c + 1) * 128], xnT[:, kc, :], kc == 0, kc == 7, ['w_in', xr[kc]], [pgk])
                    ACT(sig[c % 2][:], pg[:], AF.Sigmoid, [pgk], [('sig', c % 2)])
                    TT('dve', ubuf[:, c, 30:542], pa[:], sig[c % 2][:], ALU.mult, [pak, ('sig', c % 2)], [('ub', c)])
                    if c >= 1:
                        unit()
                for c in range(4):
                    pcv, pck = bk[4 + c % 2], ('bk', 4 + c % 2)
                    for kk in range(31):
                        MM(pcv[:], diag[:, kk * 4 + c, :], ubuf[:, c, kk:kk + 512], kk == 0, kk == 30, ['diag', ('ub', c)], [pck])
                    ACT(ycv[:, c, :], pcv[:], AF.Identity, [pck, 'colv'], [('ycv', c)], bias=col("l0_conv_b", c))
                    ACT(ysq[:, c, :], ycv[:, c, :], AF.Square, [('ycv', c)], [('ysq', c)])
                    unit()
                pmean, pmk = bk[0], ('bk', 0)
                pmsq, pqk = bk[1], ('bk', 1)
                for c in range(4):
                    MM(pmean[:], onesf[:], ycv[:, c, :], c == 0, c == 3, ['onesf', ('ycv', c)], [pmk])
                for c in range(4):
                    MM(pmsq[:], onesf[:], ysq[:, c, :], c == 0, c == 3, ['onesf', ('ysq', c)], [pqk])
                CP('dve', mean[:], pmean[:], [pmk], ['mean'])
                TT('dve', var[:], mean[:], mean[:], ALU.mult, ['mean'], ['var'])
                TT('dve', var[:], pmsq[:], var[:], ALU.subtract, [pqk, 'var'], ['var'])
                TS('dve', var[:], var[:], LN_EPS, None, ALU.add, None, ['var'], ['var'])
                ACT(var[:], var[:], AF.Sqrt, ['var'], ['var'])
                RCP(var[:], var[:], ['var'], ['var'])
                for c in range(4):
                    TT('dve', dcv[c % 2][:], ycv[:, c, :], mean[:], ALU.subtract, [('ycv', c), 'mean'], [('dcv', c % 2)])
                    TT('dve', dcv[c % 2][:], dcv[c % 2][:], var[:], ALU.mult, [('dcv', c % 2), 'var'], [('dcv', c % 2)])
                    ACT(cvT[:, c, :], dcv[c % 2][:], AF.Silu, [('dcv', c % 2), 'colv'], [('cvT', c)], scale=col("l0_conv_ln_g", c), bias=col("l0_conv_ln_b", c))
                n_ = 0
                for b in range(4):
                    for hf in range(2):
                        po, pok = bk[2 + n_ % 4], ('bk', 2 + n_ % 4)
                        n_ += 1
                        for c in range(4):
                            MM(po[:], cvT[:, c, b * 128:(b + 1) * 128], w_o[:, c, hf * 512:(hf + 1) * 512], c == 0, c == 3,
                               [('cvT', c), 'w_o'], [pok])
                        TT('dve', h[:, b, hf * 512:(hf + 1) * 512], po[:], h[:, b, hf * 512:(hf + 1) * 512], ALU.add, [pok, hk], [hk])
                        unit()
                DMA(hp_scr[i * 512:(i + 1) * 512, :].rearrange("(b p) d -> p b d", p=128), h[:], [hk], [('hp', i)], ('xst', i % 3))
        P.barrier(skip_casts=True)

        with contextlib.ExitStack() as st:
            cqnT = sbt(st, "cqnT", [128, 2, S], BF16)
            ckvnT = sbt(st, "ckvnT", [128, S], BF16)
            KT = sbt(st, "KT", [128, S], BF16)
            with contextlib.ExitStack() as s2:
                xt = [sbt(s2, "xt%d" % i, [128, 4, 1024], F32) for i in range(3)]
                xsb = [sbt(s2, "xs%d" % i, [128, 4, 1024], BF16) for i in range(2)]
                xnTb = [sbt(s2, "xnT%d" % i, [128, 8, 512], BF16) for i in range(2)]
                ssqb = [sbt(s2, "ssq%d" % i, [128, 4], F32) for i in range(2)]
                rstdb = [sbt(s2, "rstd%d" % i, [128, 4], F32) for i in range(2)]
                w_sm = sbt(s2, "w_sm", [128, 8, 416], BF16)
                rkc = sbt(s2, "rkc", [128, NB, 16], F32)
                rks = sbt(s2, "rks", [128, NB, 16], F32)
                junk = sbt(s2, "junk", [128, 256], BF16)
                ss2 = sbt(s2, "ss2", [128, 2], F32)
                cqs = sbt(s2, "cqs", [128, 4, 256], BF16)
                ckvs = sbt(s2, "ckvs", [128, 4, 128], BF16)
                kst = sbt(s2, "kst", [128, 4, 96], BF16)
                tmp = sbt(s2, "tmp", [128, 4, 16], F32)
                tpb = [pst(s2, "tpa%d" % i, [128, 1024], BF16) for i in range(2)]
                psm = [pst(s2, "psm%d" % i, [128, 512], F32) for i in range(2)]
                tq = [pst(s2, "tq%d" % i, [128, 1024], BF16) for i in range(4)]

                DMA(w_sm[:], WB["l0_w_in"][:, 1024:1440].rearrange("(k p) c -> p k c", p=128), [('wb', "l0_w_in")], ['w_sm'], 'w_sm')
                DMA(rkc[:], I["rk_c"].rearrange("(b p) d -> p b d", p=128), (), ['rkc'], 'rkc')
                DMA(rks[:], I["rk_s"].rearrange("(b p) d -> p b d", p=128), (), ['rks'], 'rks')
                MS('dve', kst[:], 0.0, ['kst'])

                def load_x2(i):
                    DMA(xt[i % 3][:], I["x"][i * 512:(i + 1) * 512, :].rearrange("(b p) d -> p b d", p=128), (), [('xt', i % 3)], ('xt', i % 3))
                def pre_a2(i):
                    return norm_stages(xt[i % 3], ('xt', i % 3), "l0_mix_norm", xsb[i % 2], xnTb[i % 2], ('xnT', i % 2), tpb,
                                       ssqb[i % 2], rstdb[i % 2], par=i % 2)
                load_x2(0)
                if NT > 1:
                    load_x2(1)
                for f in pre_a2(0):
                    f()
                for i in range(NT):
                    if i + 2 < NT:
                        load_x2(i + 2)
                    h = xt[i % 3]
                    hk = ('xt', i % 3)
                    xnT = xnTb[i % 2]
                    xr = [(('xnT', i % 2), kc) for kc in range(8)]
                    stg_it = iter(pre_a2(i + 1) if i + 1 < NT else [])

                    def unit(n=1):
                        for _ in range(n):
                            f = next(stg_it, None)
                            if f is not None:
                                f()
                    for b in range(4):
                        pp = psm[b % 2]
                        pk = ('psm', b % 2)
                        gb = i * 4 + b
                        for kc in range(8):
                            MM(pp[:, 0:416], xnT[:, kc, b * 128:(b + 1) * 128], w_sm[:, kc, :], kc == 0, kc == 7, [xr[kc], 'w_sm'], [pk])
                        ACT(junk[:, 0:256], pp[:, 0:256], AF.Square, [pk], ['junk', 'ss2'], accum=ss2[:, 0:1])
                        ACT(junk[:, 0:128], pp[:, 256:384], AF.Square, [pk], ['junk', 'ss2'], accum=ss2[:, 1:2])
                        TS('dve', ss2[:, 0:1], ss2[:, 0:1], 1.0 / 256, EPS, ALU.mult, ALU.add, ['ss2'], ['ss2'])
                        TS('dve', ss2[:, 1:2], ss2[:, 1:2], 1.0 / 128, EPS, ALU.mult, ALU.add, ['ss2'], ['ss2'])
                        ACT(ss2[:], ss2[:], AF.Sqrt, ['ss2'], ['ss2'])
                        RCP(ss2[:], ss2[:], ['ss2'], ['ss2'])
                        TS('dve', cqs[:, b, :], pp[:, 0:256], ss2[:, 0:1], None, ALU.mult, None, [pk, 'ss2'], ['cqs'])
                        TS('dve', ckvs[:, b, :], pp[:, 256:384], ss2[:, 1:2], None, ALU.mult, None, [pk, 'ss2'], ['ckvs'])
                        TT('dve', tmp[:, 0, :], pp[:, 384:400], rkc[:, gb, :], ALU.mult, [pk, 'rkc'], ['tmp'])
                        TT('dve', tmp[:, 1, :], pp[:, 400:416], rks[:, gb, :], ALU.mult, [pk, 'rks'], ['tmp'])
                        TT('dve', tmp[:, 2, :], pp[:, 384:400], rks[:, gb, :], ALU.mult, [pk, 'rks'], ['tmp'])
                        TT('dve', tmp[:, 3, :], pp[:, 400:416], rkc[:, gb, :], ALU.mult, [pk, 'rkc'], ['tmp'])
                        TT('dve', kst[:, b, 64:80], tmp[:, 0, :], tmp[:, 1, :], ALU.subtract, ['tmp'], ['kst'])
                        TT('dve', kst[:, b, 80:96], tmp[:, 2, :], tmp[:, 3, :], ALU.add, ['tmp'], ['kst'])
                        unit(2)
                    tsl = slice(i * 512, (i + 1) * 512)
                    for b in range(4):
                        TR(tq[0][:, b * 128:(b + 1) * 128], cqs[:, b, 0:128], ident[:], ['cqs', 'ident'], [('tq', 0)])
                        TR(tq[1][:, b * 128:(b + 1) * 128], cqs[:, b, 128:256], ident[:], ['cqs', 'ident'], [('tq', 1)])
                        TR(tq[2][:, b * 128:(b + 1) * 128], ckvs[:, b, :], ident[:], ['ckvs', 'ident'], [('tq', 2)])
                        TR(tq[3][0:96, b * 128:(b + 1) * 128], kst[:, b, :], ident[:], ['kst', 'ident'], [('tq', 3)])
                    ACT(cqnT[:, 0, tsl], tq[0][:, 0:512], AF.Copy, [('tq', 0), 'colv'], ['cqnT'], scale=col("l0_q_norm", 0))
                    ACT(cqnT[:, 1, tsl], tq[1][:, 0:512], AF.Copy, [('tq', 1), 'colv'], ['cqnT'], scale=col("l0_q_norm", 1))
                    ACT(ckvnT[:, tsl], tq[2][:, 0:512], AF.Copy, [('tq', 2), 'colv'], ['ckvnT'], scale=col("l0_kv_norm", 0))
                    CP('dve', KT[64:96, tsl], tq[3][64:96, 0:512], [('tq', 3)], ['KTr'])
                    unit(4)
            P.barrier(skip_casts=True)

            with contextlib.ExitStack() as s2:
                rqc = sbt(s2, "rqc", [128, S], BF16)
                rqs = sbt(s2, "rqs", [128, S], BF16)
                w_uq = sbt(s2, "w_uq", [128, 2, 768], BF16)
                w_uqs = sbt(s2, "w_uqs", [128, 2, 768], BF16)
                w_ukv = sbt(s2, "w_ukv", [128, 1024], BF16)
                Vb = [sbt(s2, "Vb%d" % i, [128, NB, 128], BF16) for i in range(2)]
                QT = [sbt(s2, "QT%d" % i, [128, 512], BF16) for i in range(2)]
                PT = [sbt(s2, "PT%d" % i, [128, 512], BF16) for i in range(4)]
                tri = sbt(s2, "tri", [128, 128], BF16)
                trif = sbt(s2, "trif", [128, 128], F32)
                osb = [sbt(s2, "osb%d" % i, [128, 512], F32) for i in range(2)]
                rinv = sbt(s2, "rinv", [128, 512], F32)
                aout = [sbt(s2, "aout%d" % i, [128, 512], BF16) for i in range(2)]
                qtmp = sbt(s2, "qtmp", [128, 2, 512], F32)
                sel = [sbt(s2, "sel%d" % i, [128, 128], F32) for i in range(2)]
                pq = pst(s2, "pq", [128, 512], F32)
                pqs = pst(s2, "pqs", [128, 512], F32)
                pss = [pst(s2, "pss%d" % i, [128, 512], F32) for i in range(3)]
                pov = [pst(s2, "pov%d" % i, [128, 512], F32) for i in range(2)]
                pkv = pst(s2, "pkv", [128, 512], F32)
                prs = pkv

                DMA(rqc[64:96, :], I["rq_c"], (), ['rqc'], 'rqc', eng='pool', maxlast=4096)
                DMA(rqs[64:96, :], I["rq_s"], (), ['rqs'], 'rqs', eng='pool', maxlast=4096)
                DMA(w_uq[:], WB["l0_w_uq"].rearrange("(k p) c -> p k c", p=128), [('wb', "l0_w_uq")], ['w_uq'], 'w_uq')
                DMA(w_uqs[:], WB["w_uq_sw"].rearrange("(k p) c -> p k c", p=128), [('wb', "w_uq_sw")], ['w_uqs'], 'w_uqs')
                DMA(w_ukv[:], WB["l0_w_ukv"], [('wb', "l0_w_ukv")], ['w_ukv'], 'w_ukv')
                MS('pool', trif[:], 1.0, ['trif'])
                P.op('pool', lambda e: e.affine_select(out=trif[:], in_=trif[:], pattern=[[1, 128]], compare_op=ALU.is_ge,
                                                       fill=0.0, base=0, channel_multiplier=-1), ['trif'], ['trif'])
                CP('dve', tri[:], trif[:], ['trif'], ['tri'])
                MS('pool', sel[0][:], 0.0, ['sel0'])
                MS('pool', sel[1][:], 0.0, ['sel1'])
                MS('pool', sel[0][64:65, 0:64], 1.0, ['sel0'])
                MS('pool', sel[1][0:1, 64:128], 1.0, ['sel1'])
                MS('pool', Vb[0][:], 0.0, [('V', 0)])
                MS('pool', Vb[1][:], 0.0, [('V', 1)])
                MS('pool', Vb[0][:, :, 64:65], 1.0, [('V', 0)])
                MS('pool', Vb[1][:, :, 0:1], 1.0, [('V', 1)])
                qscale = 96.0 ** -0.5

                for hd in range(8):
                    par = hd % 2
                    V = Vb[par]
                    vk = ('V', par)
                    rlo = 64 * par
                    for kt in range(NT):
                        ksl = slice(kt * 512, (kt + 1) * 512)
                        MM(pkv[0:64, :], w_ukv[:, hd * 128:hd * 128 + 64], ckvnT[:, ksl], True, True, ['w_ukv', 'ckvnT'], ['pkv'])
                        ACT(KT[0:64, ksl], pkv[0:64, :], AF.Copy, ['pkv'], ['KTn'])
                    for kg in range(NB // 8):
                        for j in range(8):
                            kb = kg * 8 + j
                            MM(pkv[:, j * 64:(j + 1) * 64], ckvnT[:, kb * 128:(kb + 1) * 128], w_ukv[:, hd * 128 + 64:hd * 128 + 128],
                               True, True, ['ckvnT', 'w_ukv'], ['pkv'])
                        CP('dve', V[:, kg * 8:(kg + 1) * 8, rlo:rlo + 64], pkv[:].rearrange("p (j d) -> p j d", d=64), ['pkv'], [vk])
                    def emit_Q(qt):
                        qsl = slice(qt * 512, (qt + 1) * 512)
                        Q = QT[qt % 2]
                        qk = ('QT', qt % 2)
                        for kc in range(2):
                            MM(pq[0:96, :], w_uq[:, kc, hd * 96:(hd + 1) * 96], cqnT[:, kc, qsl], kc == 0, kc == 1, ['w_uq', 'cqnT'], ['pq'])
                        for kc in range(2):
                            MM(pqs[0:96, :], w_uqs[:, kc, hd * 96:(hd + 1) * 96], cqnT[:, kc, qsl], kc == 0, kc == 1, ['w_uqs', 'cqnT'], ['pqs'])
                        ACT(Q[0:64, :], pq[0:64, :], AF.Copy, ['pq'], [qk], scale=qscale)
                        TT('dve', qtmp[64:96, 0, :], pq[64:96, :], rqc[64:96, qsl], ALU.mult, ['pq', 'rqc'], ['qtmp'])
                        TT('dve', qtmp[64:96, 1, :], pqs[64:96, :], rqs[64:96, qsl], ALU.mult, ['pqs', 'rqs'], ['qtmp'])
                        TT('dve', Q[64:96, :], qtmp[64:96, 0, :], qtmp[64:96, 1, :], ALU.add, ['qtmp'], [qk])

                    def emit_S(qt, kb):
                        Q = QT[qt % 2]
                        qk = ('QT', qt % 2)
                        dg = kb - qt * 4
                        c0 = max(dg, 0) * 128
                        ps_ = pss[kb % 3]
                        psk = ('pss', kb % 3)
                        pt = PT[kb % 4]
                        ptk = ('PT', kb % 4)
                        MM(ps_[:, c0:512], KT[0:96, kb * 128:(kb + 1) * 128], Q[0:96, c0:512], True, True, ['KTn', 'KTr', qk], [psk])
                        ACT(pt[:, c0:512], ps_[:, c0:512], AF.Exp, [psk], [ptk])
                        if dg >= 0:
                            TT('dve', pt[:, c0:c0 + 128], pt[:, c0:c0 + 128], tri[:], ALU.mult, [ptk, 'tri'], [ptk])

                    def emit_PV(qt, kb, nkb):
                        dg = kb - qt * 4
                        c0 = max(dg, 0) * 128
                        MM(pov[qt % 2][:, c0:512], V[:, kb, :], PT[kb % 4][:, c0:512], kb == 0, kb == nkb - 1,
                           [vk, ('PT', kb % 4)], [('pov', qt % 2)])

                    def emit_epi(qt):
                        qsl = slice(qt * 512, (qt + 1) * 512)
                        ob = osb[qt % 2]
                        obk = ('osb', qt % 2)
                        MM(prs[:], sel[par][:], ob[:], True, True, ['sel%d' % par, obk], ['pkv'])
                        RCP(rinv[rlo:rlo + 64, :], prs[rlo:rlo + 64, :], ['pkv'], ['rinv'])
                        ao = aout[qt % 2]
                        aok = ('aout', qt % 2)
                        TT('dve', ao[rlo:rlo + 64, :], ob[rlo:rlo + 64, :], rinv[rlo:rlo + 64, :], ALU.mult, [obk, 'rinv'], [aok])
                        DMA(att_scr[hd * 64:(hd + 1) * 64, qsl], ao[rlo:rlo + 64, :], [aok], [('att', qt)], ('aout', qt % 2))

                    emit_Q(0)
                    pending = None
                    for qt in range(NT):
                        nkb = (qt + 1) * 4
                        if qt + 1 < NT:
                            emit_Q(qt + 1)
                        emit_S(qt, 0)
                        emit_S(qt, 1)
                        emit_S(qt, 2)
                        if pending is not None:
                            emit_epi(pending)
                        for kb in range(nkb):
                            emit_PV(qt, kb, nkb)
                            if kb + 3 < nkb:
                                emit_S(qt, kb + 3)
                        ACT(osb[qt % 2][:], pov[qt % 2][:], AF.Copy, [('pov', qt % 2)], [('osb', qt % 2)])
                        pending = qt
                    emit_epi(pending)
        P.barrier()

        def ffn(L, xnT, h, hk, fb, ti, xnkey='xnT', hook=None, last=False):
            wu, wd, acc, hact, sil, pu, pd = fb
            pre = "l%d_" % L
            wup = WB[pre + "w_up"]
            wdn = WB[pre + "w_down"]
            cw = pre + "ffn_conv_w"
            cb = pre + "ffn_conv_b"
            xr = [(xnkey, kc) for kc in range(8)]
            cur, nxt = ti % 2, (ti + 1) % 2

            def load_wu(m):
                s_ = m % 3
                DMA(wu[s_][:].rearrange("p k c -> p (k c)"), wup[m], [('wb', pre + "w_up")], [('wu', s_)], ('wu', s_))

            def load_wd(q):
                DMA(wd[q][:].rearrange("p k c -> p (k c)"), wdn[q], [('wb', pre + "w_down")], [('wd', q)], ('wd', q))
            if ti == 0:
                load_wu(0)
                load_wu(1)
            hall = [('halo', L, cur, o_) for o_ in range(44)]
            w0c = colv[:, COL[cw]:COL[cw] + 44]
            w1c = colv[:, COL[cw] + 44:COL[cw] + 88]
            hc = halo[:, L, cur]
            TT('dve', corr[:, :, 1], hc[:, :, 1], w0c, ALU.mult, hall + ['colv'], ['corr'])
            TT('dve', ctm[:, :], hc[:, :, 0], w0c, ALU.mult, hall + ['colv'], ['ctm'])
            TT('dve', corr[:, :, 0], hc[:, :, 1], w1c, ALU.mult, hall + ['colv'], ['corr'])
            TT('dve', corr[:, :, 0], corr[:, :, 0], ctm[:, :], ALU.add, ['corr', 'ctm'], ['corr'])
            for m in range(11):
                if m + 2 < 11:
                    load_wu(m + 2)
                if m in (1, 3, 5, 7):
                    load_wd((m - 1) // 2)
                w = wu[m % 3]
                wk = ('wu', m % 3)
                for jj in range(2):
                    j = m * 2 + jj
                    if hook is not None:
                        hook(j)
                    for half in range(2):
                        ot = j + 22 * half
                        bi = jj * 2 + half
                        pp = pu[bi]
                        ppk = ('pu', bi)
                        ac = acc[bi]
                        ack = ('acc', bi)
                        cofs = half * 256 + jj * 128
                        hcur = ('halo', L, cur, ot)
                        hnxt = ('halo', L, nxt, ot)
                        for kc in range(8):
                            MM(pp[:], w[:, kc, cofs:cofs + 128], xnT[:, kc, :], kc == 0, kc == 7, [wk, xr[kc]], [ppk])
                        ACT(ac[:], pp[:], AF.Identity, [ppk, 'colv'], [ack], scale=col(cw, 2 * 44 + ot), bias=col(cb, ot))
                        ACT(halo[:, L, nxt, ot, :], pp[:, 510:512], AF.Copy, [ppk], [hnxt])
                        STT(ac[:, 1:512], pp[:, 0:511], col(cw, 1 * 44 + ot), ac[:, 1:512], ALU.mult, ALU.add, [ppk, 'colv', ack], [ack])
                        STT(ac[:, 2:512], pp[:, 0:510], col(cw, 0 * 44 + ot), ac[:, 2:512], ALU.mult, ALU.add, [ppk, 'colv', ack], [ack])
                        TT('dve', ac[:, 0:2], ac[:, 0:2], corr[:, ot, :], ALU.add, ['corr', ack], [ack])
                    ACT(sil[jj][:], acc[jj * 2][:], AF.Silu, [('acc', jj * 2)], [('sil', jj)])
                    TT('pool', hact[:, j, :], sil[jj][:], acc[jj * 2 + 1][:], ALU.mult, [('sil', jj), ('acc', jj * 2 + 1)], [('hact', j)])
            if not last:
                load_wu(0)
                load_wu(1)
            har = [('hact', j) for j in range(22)]
            for q in range(4):
                w = wd[q]
                wk = ('wd', q)
                for b in range(4):
                    pp = pd[b % 2]
                    ppk = ('pd', b % 2)
                    for j in range(22):
                        MM(pp[:, 0:256], hact[:, j, b * 128:(b + 1) * 128], w[:, j, :], j == 0, j == 21, [har[j], wk], [ppk])
                    TT('dve', h[:, b, q * 256:(q + 1) * 256], pp[:, 0:256], h[:, b, q * 256:(q + 1) * 256], ALU.add, [ppk, hk], [hk])

        def ffn_bufs(st):
            wu = [sbt(st, "wu%d" % i, [128, 8, 512], BF16) for i in range(3)]
            wd = [sbt(st, "wd%d" % i, [128, 22, 256], BF16) for i in range(4)]
            acc = [sbt(st, "acc%d" % i, [128, 512], F32) for i in range(4)]
            hact = sbt(st, "hact", [128, 22, 512], BF16)
            sil = [sbt(st, "sil%d" % i, [128, 512], F32) for i in range(2)]
            pu = [pst(st, "pu%d" % i, [128, 512], F32) for i in range(4)]
            pd = [pst(st, "pd%d" % i, [128, 512], F32) for i in range(2)]
            return wu, wd, acc, hact, sil, pu, pd

        halo = sbt(top, "halo", [128, 2, 2, 44, 2], F32)
        corr = sbt(top, "corr", [128, 44, 2], F32)
        ctm = sbt(top, "ctm", [128, 44], F32)
        MS('dve', halo[:], 0.0, [('halo', L_, p_, o_) for L_ in range(2) for p_ in range(2) for o_ in range(44)])

        with contextlib.ExitStack() as st:
            htb = [sbt(st, "ht%d" % i, [128, 4, 1024], F32) for i in range(3)]
            attb = [sbt(st, "att%d" % i, [128, 4, 512], BF16) for i in range(2)]
            xsb = [sbt(st, "xs%d" % i, [128, 4, 1024], BF16) for i in range(2)]
            xnTb = [sbt(st, "xnT%d" % i, [128, 8, 512], BF16) for i in range(2)]
            ssqb = [sbt(st, "ssq%d" % i, [128, 4], F32) for i in range(2)]
            rstdb = [sbt(st, "rstd%d" % i, [128, 4], F32) for i in range(2)]
            w_o = sbt(st, "w_o", [128, 4, 1024], BF16)
            tpb = [pst(st, "tpa%d" % i, [128, 1024], BF16) for i in range(2)]
            fb = ffn_bufs(st)
            pd_ = fb[6]
            DMA(w_o[:], WB["l0_w_out"][512:1024, :].rearrange("(k p) c -> p k c", p=128), [('wb', "l0_w_out")], ['w_o'], 'w_o')

            def load_b1(i):
                tsl = slice(i * 512, (i + 1) * 512)
                DMA(htb[i % 3][:], hp_scr[tsl, :].rearrange("(b p) d -> p b d", p=128), [('hp', i)], [('ht', i % 3)], ('ht', i % 3))
                DMA(attb[i % 2][:], att_scr[:, tsl].rearrange("(c p) t -> p c t", p=128), [('att', i)], [('attb', i % 2)], ('attb', i % 2))

            def pre_b1(i):
                tsl = slice(i * 512, (i + 1) * 512)
                ht = htb[i % 3]
                hk = ('ht', i % 3)
                att = attb[i % 2]
                ak = ('attb', i % 2)

                def wo(n_):
                    b, hf = n_ // 2, n_ % 2
                    po = pd_[n_ % 2]
                    pok = ('pd', n_ % 2)
                    for c in range(4):
                        MM(po[:], att[:, c, b * 128:(b + 1) * 128], w_o[:, c, hf * 512:(hf + 1) * 512], c == 0, c == 3, [ak, 'w_o'], [pok])
                    TT('dve', ht[:, b, hf * 512:(hf + 1) * 512], po[:], ht[:, b, hf * 512:(hf + 1) * 512], ALU.add, [pok, hk], [hk])
                    if dbg and n_ == 7:
                        DMA(hp_scr[tsl, :].rearrange("(b p) d -> p b d", p=128), ht[:], [hk], [('hpd', i)], 'hst_d')
                ns = norm_stages(ht, hk, "l0_ffn_norm", xsb[i % 2], xnTb[i % 2], ('xnT', i % 2), tpb, ssqb[i % 2], rstdb[i % 2], par=i % 2)
                return [(lambda n_=n_: wo(n_)) for n_ in range(8)] + ns

            load_b1(0)
            if NT > 1:
                load_b1(1)
            for f in pre_b1(0):
                f()
            for i in range(NT):
                if i + 2 < NT:
                    load_b1(i + 2)
                tsl = slice(i * 512, (i + 1) * 512)
                ht = htb[i % 3]
                hk = ('ht', i % 3)
                hook = None
                if i + 1 < NT:
                    stg = pre_b1(i + 1)
                    hook = (lambda k, stg=stg: stg[k - 3]() if 3 <= k < 3 + len(stg) else None)
                ffn(0, xnTb[i % 2], ht, hk, fb, i, xnkey=('xnT', i % 2), hook=hook, last=(i == NT - 1))
                DMA(h1_scr[tsl, :].rearrange("(b p) d -> p b d", p=128), ht[:], [hk], [('h1', i)], ('hst', i % 3))
        P.barrier()

        TWO_PI = 2.0 * math.pi
        with contextlib.ExitStack() as st:
            htb = [sbt(st, "ht%d" % i, [128, 4, 1024], F32) for i in range(2)]
            xs = sbt(st, "xs", [128, 4, 1024], BF16)
            xnT = sbt(st, "xnT", [128, 8, 512], BF16)
            ssq = sbt(st, "ssq", [128, 4], F32)
            rstd = sbt(st, "rstd", [128, 4], F32)
            ns512 = sbt(st, "ns512", [128, 16], F32)
            w_in1 = sbt(st, "w_in1", [128, 8, 512], BF16)
            w_glu = sbt(st, "w_glu", [128, 4, 2048], BF16)
            bglu_f = sbt(st, "bglu_f", [1, 2048], F32)
            bglu = sbt(st, "bglu", [1, 2048], BF16)
            ones_r = sbt(st, "ones_r", [1, 128], BF16)
            BbT = sbt(st, "BbT", [128, 32, 128], BF16)
            CTt = sbt(st, "CTt", [128, 48, 128], BF16)
            cosT = sbt(st, "cosT", [128, 16, 512], BF16)
            sinT = sbt(st, "sinT", [128, 16, 512], BF16)
            c512 = sbt(st, "c512", [128, 16], F32)
            s512 = sbt(st, "s512", [128, 16], F32)
            rmag = sbt(st, "rmag", [128, 16], F32)
            car = sbt(st, "car", [128, 2, 16], F32)
            ctmp = sbt(st, "ctmp", [128, 4], F32)
            tpb = [pst(st, "tpa%d" % i, [128, 1024], BF16) for i in range(2)]
            pbr = [pst(st, "pbr%d" % i, [128, 512], F32) for i in range(2)]
            pbi = [pst(st, "pbi%d" % i, [128, 512], F32) for i in range(2)]
            py = [pst(st, "py%d" % i, [128, 512], F32) for i in range(2)]
            pz = py[1]

            with contextlib.ExitStack() as s2:
                ldb = sbt(s2, "ldb", [128, 32], F32)
                dtc = sbt(s2, "dtc", [128, 16], F32)
                pr_ = [sbt(s2, "pr%d" % i, [128, 16], F32) for i in range(12)]
                Bs = [sbt(s2, "Bs%d" % i, [128, 16, 16], F32) for i in range(2)]
                Bb = [sbt(s2, "Bb%d" % i, [128, 16, 16], F32) for i in range(2)]
                Bt = sbt(s2, "Bt", [128, 16, 16], F32)
                Bpad = sbt(s2, "Bpad", [128, 128], BF16)
                Xc = [sbt(s2, "Xc%d" % i, [128, 4, 128], F32) for i in range(2)]
                Xcb = sbt(s2, "Xcb", [128, 128], BF16)
                XcT = sbt(s2, "XcT", [128, 128], BF16)
                io_i = sbt(s2, "io_i", [128, 513], I32)
                io_f = sbt(s2, "io_f", [128, 513], F32)
                ph = sbt(s2, "ph", [128, 513], F32)
                ph2 = sbt(s2, "ph2", [128, 513], F32)
                ph_i = sbt(s2, "ph_i", [128, 513], I32)
                tbl = sbt(s2, "tbl", [128, 513], F32)

                DMA(ldb[:], I["l1_log_dt"].partition_broadcast(128), (), ['ldb'], 'ldb')
                ldv = ldb[:].rearrange("p (j g) -> p j g", g=2)
                CP('dve', dtc[0:64, :], ldv[0:64, :, 0], ['ldb'], ['dtc'])
                CP('dve', dtc[64:128, :], ldv[64:128, :, 1], ['ldb'], ['dtc'])
                ACT(dtc[:], dtc[:], AF.Exp, ['dtc'], ['dtc'])
                a_re = col("l1_a_re", 0, 16)
                a_im = col("l1_a_im", 0, 16)
                ardt, fturn, mag, sinv, cosv, lbre, lbim, den, fre, fim, t0, t1 = pr_
                K = ['prm']
                TT('dve', ardt[:], a_re, dtc[:], ALU.mult, ['colv', 'dtc'], K)
                ACT(mag[:], ardt[:], AF.Exp, K, K)
                CP('dve', rmag[:], mag[:], K, ['rmag'])
                TT('dve', fturn[:], a_im, dtc[:], ALU.mult, ['colv', 'dtc'], K)
                TS('dve', fturn[:], fturn[:], 1.0 / TWO_PI, None, ALU.mult, None, K, K)

                def sincos_turns(src, n, dst_sin, dst_cos, keys):
                    for shift, dst in ((0.0, dst_sin), (0.25, dst_cos)):
                        TS('dve', ph2[:, 0:n], src, shift, None, ALU.add, None, keys, ['ph2'])
                        CP('dve', ph_i[:, 0:n], ph2[:, 0:n], ['ph2'], ['ph_i'])
                        CP('dve', tbl[:, 0:n], ph_i[:, 0:n], ['ph_i'], ['tbl'])
                        TT('dve', ph2[:, 0:n], ph2[:, 0:n], tbl[:, 0:n], ALU.subtract, ['ph2', 'tbl'], ['ph2'])
                        TS('dve', ph2[:, 0:n], ph2[:, 0:n], 0.49999, -0.49999, ALU.min, ALU.max, ['ph2'], ['ph2'])
                        ACT(dst, ph2[:, 0:n], AF.Sin, ['ph2'], keys, scale=TWO_PI)
                sincos_turns(fturn[:], 16, sinv[:], cosv[:], K)
                TT('dve', lbre[:], mag[:], cosv[:], ALU.mult, K, K)
                TT('dve', lbim[:], mag[:], sinv[:], ALU.mult, K, K)
                TT('dve', den[:], a_re, a_re, ALU.mult, ['colv'], K)
                TT('dve', t0[:], a_im, a_im, ALU.mult, ['colv'], K)
                TT('dve', den[:], den[:], t0[:], ALU.add, K, K)
                RCP(den[:], den[:], K, K)
                TS('dve', lbre[:], lbre[:], -1.0, None, ALU.add, None, K, K)
                TT('dve', t0[:], lbre[:], a_re, ALU.mult, K + ['colv'], K)
                TT('dve', t1[:], lbim[:], a_im, ALU.mult, K + ['colv'], K)
                TT('dve', fre[:], t0[:], t1[:], ALU.add, K, K)
                TT('dve', fre[:], fre[:], den[:], ALU.mult, K, K)
                TT('dve', t0[:], lbim[:], a_re, ALU.mult, K + ['colv'], K)
                TT('dve', t1[:], lbre[:], a_im, ALU.mult, K + ['colv'], K)
                TT('dve', fim[:], t0[:], t1[:], ALU.subtract, K, K)
                TT('dve', fim[:], fim[:], den[:], ALU.mult, K, K)
                DMA(Bs[0][:], I["l1_b_re"].rearrange("g p c -> (g p) c").rearrange("(j s) c -> s j c", s=128), (), ['Bs0'], 'Bs0')
                DMA(Bs[1][:], I["l1_b_im"].rearrange("g p c -> (g p) c").rearrange("(j s) c -> s j c", s=128), (), ['Bs1'], 'Bs1')
                freb = fre[:].unsqueeze(2).to_broadcast([128, 16, 16])
                fimb = fim[:].unsqueeze(2).to_broadcast([128, 16, 16])
                TT('dve', Bb[0][:], Bs[0][:], freb, ALU.mult, ['Bs0'] + K, ['Bb0'])
                TT('dve', Bt[:], Bs[1][:], fimb, ALU.mult, ['Bs1'] + K, ['Bt'])
                TT('dve', Bb[0][:], Bb[0][:], Bt[:], ALU.subtract, ['Bb0', 'Bt'], ['Bb0'])
                TT('dve', Bb[1][:], Bs[1][:], freb, ALU.mult, ['Bs1'] + K, ['Bb1'])
                TT('dve', Bt[:], Bs[0][:], fimb, ALU.mult, ['Bs0'] + K, ['Bt'])
                TT('dve', Bb[1][:], Bb[1][:], Bt[:], ALU.add, ['Bb1', 'Bt'], ['Bb1'])
                for j in range(16):
                    for ri in range(2):
                        MS('pool', Bpad[:], 0.0, ['Bpad'])
                        base = (j % 4) * 32
                        CP('dve', Bpad[0:64, base:base + 16], Bb[ri][0:64, j, :], ['Bb%d' % ri, 'Bpad'], ['Bpad'])
                        CP('dve', Bpad[64:128, base + 16:base + 32], Bb[ri][64:128, j, :], ['Bb%d' % ri, 'Bpad'], ['Bpad'])
                        TR(tpb[0][:, 0:128], Bpad[:], ident[:], ['Bpad', 'ident'], [('tp', 0)])
                        CP('dve', BbT[:, j * 2 + ri, :], tpb[0][:, 0:128], [('tp', 0)], ['BbT'])
                for ri, nm in enumerate(["l1_c_re", "l1_c_im"]):
                    MS('pool', Xc[ri][:], 0.0, ['Xc%d' % ri])
                    for g in range(32):
                        ct, g8 = g // 8, g % 8
                        gl = g8 % 2
                        DMA(Xc[ri][g8 * 16:(g8 + 1) * 16, ct, gl * 64:(gl + 1) * 64], I[nm][g], ['Xc%d' % ri], [('Xcg', ri, g)], 'Xc%d' % ri)
                MS('pool', CTt[:], 0.0, ['CTt'])
                for var_i, (ri, sgn) in enumerate(((0, 1.0), (1, -1.0), (0, -1.0))):
                    for ct in range(4):
                        TS('dve', Xcb[:], Xc[ri][:, ct, :], sgn, None, ALU.mult, None,
                           ['Xc%d' % ri] + [('Xcg', ri, g) for g in range(32)], ['Xcb'])
                        TR(tpb[1][:, 0:128], Xcb[:], ident[:], ['Xcb', 'ident'], [('tp', 1)])
                        for jj in range(4):
                            j = ct * 4 + jj
                            CP('dve', CTt[:, j * 3 + var_i, jj * 32:(jj + 1) * 32], tpb[1][:, jj * 32:(jj + 1) * 32], [('tp', 1)], ['CTt'])
                P.op('pool', lambda e: e.iota(io_i[:], [[1, 513]], base=0, channel_multiplier=0), (), ['io_i'])
                CP('dve', io_f[:], io_i[:], ['io_i'], ['io_f'])
                for j in range(16):
                    TS('dve', ph[:], io_f[:], fturn[:, j:j + 1], None, ALU.mult, None, ['io_f'] + K, ['ph'])
                    for shift, dstT, dst512 in ((0.0, sinT, s512), (0.25, cosT, c512)):
                        TS('dve', ph2[:], ph[:], shift, None, ALU.add, None, ['ph'], ['ph2'])
                        CP('dve', ph_i[:], ph2[:], ['ph2'], ['ph_i'])
                        CP('dve', tbl[:], ph_i[:], ['ph_i'], ['tbl'])
                        TT('dve', ph2[:], ph2[:], tbl[:], ALU.subtract, ['ph2', 'tbl'], ['ph2'])
                        TS('dve', ph2[:], ph2[:], 0.49999, -0.49999, ALU.min, ALU.max, ['ph2'], ['ph2'])
                        ACT(tbl[:], ph2[:], AF.Sin, ['ph2'], ['tbl'], scale=TWO_PI)
                        CP('dve', dstT[:, j, :], tbl[:, 0:512], ['tbl'], ['tabs'])
                        CP('dve', dst512[:, j:j + 1], tbl[:, 512:513], ['tbl'], ['tabs'])
            P.barrier()
            uT = sbt(st, "uT", [128, 4, 512], BF16)
            zt = [sbt(st, "zt%d" % i, [128, 512], F32) for i in range(4)]
            ztb = [sbt(st, "ztb%d" % i, [128, 512], BF16) for i in range(8)]
            nident = sbt(st, "nident", [128, 128], BF16)
            wre = [sbt(st, "wre%d" % i, [128, 512], F32) for i in range(2)]
            wim = [sbt(st, "wim%d" % i, [128, 512], F32) for i in range(2)]
            ot_ = [sbt(st, "ot%d" % i, [128, 512], BF16) for i in range(8)]
            yv, g1, g2, sg = zt
            yT = sbt(st, "yT", [128, 4, 512], BF16)
            DMA(w_in1[:], WB["l1_w_in"].rearrange("(k p) c -> p k c", p=128), [('wb', "l1_w_in")], ['w_in1'], 'w_in1')
            DMA(w_glu[:], WB["l1_w_glu"].rearrange("(k p) c -> p k c", p=128), [('wb', "l1_w_glu")], ['w_glu'], 'w_glu')
            DMA(bglu_f[:], I["l1_b_glu"].rearrange("(o c) -> o c", o=1), (), ['bglu_f'], 'bglu_f')
            CP('dve', bglu[:], bglu_f[:], ['bglu_f'], ['bglu'])
            MS('pool', ones_r[:], 1.0, ['ones_r'])
            MS('pool', car[:], 0.0, [('car', j_) for j_ in range(16)])
            TS('dve', nident[:], ident[:], -1.0, None, ALU.mult, None, ['ident'], ['nident'])
            TS('dve', ns512[:], s512[:], -1.0, None, ALU.mult, None, ['tabs'], ['tabs'])

            def load_b2(i):
                DMA(htb[i % 2][:], h1_scr[i * 512:(i + 1) * 512, :].rearrange("(b p) d -> p b d", p=128), [('h1', i)], [('ht', i % 2)], ('ht', i % 2))
            load_b2(0)
            for i in range(NT):
                if i + 1 < NT:
                    load_b2(i + 1)
                tsl = slice(i * 512, (i + 1) * 512)
                ht = htb[i % 2]
                hk = ('ht', i % 2)
                norm_T("b2", ht, hk, "l1_mix_norm", xs, xnT, 'xnT', tpb, ssq, rstd)
                xr = [('xnT', kc) for kc in range(8)]
                def emit_u(c, bank):
                    pzb = py[bank]
                    pzk = ('py', bank)
                    for kc in range(8):
                        MM(pzb[:], w_in1[:, kc, c * 128:(c + 1) * 128], xnT[:, kc, :], kc == 0, kc == 7, ['w_in1', xr[kc]], [pzk])
                    ACT(uT[:, c, :], pzb[:], AF.Copy, [pzk], [('uT', c)])
                emit_u(0, 1)
                def emit_b(j):
                    c = j // 4
                    s = j % 2
                    uk = ('uT', c)
                    MM(pbr[s][:], BbT[:, j * 2, :], uT[:, c, :], True, True, ['BbT', uk], [('pbr', s)])
                    MM(pbi[s][:], BbT[:, j * 2 + 1, :], uT[:, c, :], True, True, ['BbT', uk], [('pbi', s)])

                def emit_zscan(j):
                    s = j % 2
                    cs = cosT[:, j, :]
                    sn = sinT[:, j, :]
                    z4 = [ztb[s * 4 + k] for k in range(4)]
                    zk4 = [('ztb', s * 4 + k) for k in range(4)]
                    TT('dve', z4[0][:], pbr[s][:], cs, ALU.mult, [('pbr', s), 'tabs'], [zk4[0]])
                    TT('dve', z4[1][:], pbi[s][:], sn, ALU.mult, [('pbi', s), 'tabs'], [zk4[1]])
                    TT('dve', z4[2][:], pbi[s][:], cs, ALU.mult, [('pbi', s), 'tabs'], [zk4[2]])
                    TT('dve', z4[3][:], pbr[s][:], sn, ALU.mult, [('pbr', s), 'tabs'], [zk4[3]])
                    zre = tpb[0][:].bitcast(F32)
                    zim = tpb[1][:].bitcast(F32)
                    MM(zre, ident[:], z4[0][:], True, False, ['ident', zk4[0]], [('tp', 0)])
                    MM(zre, ident[:], z4[1][:], False, True, ['ident', zk4[1]], [('tp', 0)])
                    MM(zim, ident[:], z4[2][:], True, False, ['ident', zk4[2]], [('tp', 1)])
                    MM(zim, nident[:], z4[3][:], False, True, ['nident', zk4[3]], [('tp', 1)])
                    rb = rmag[:, j:j + 1].to_broadcast([128, 512])
                    SCAN(wre[s][:], rb, zre, car[:, 0, j:j + 1], ['rmag', ('tp', 0), ('car', j)], [('wre', s)])
                    SCAN(wim[s][:], rb, zim, car[:, 1, j:j + 1], ['rmag', ('tp', 1), ('car', j)], [('wim', s)])
                    ACT(ctmp[:, 0:1], wim[s][:, 511:512], AF.Identity, [('wim', s), 'tabs'], ['ctmp0'], scale=ns512[:, j:j + 1])
                    ACT(car[:, 0, j:j + 1], wre[s][:, 511:512], AF.Identity, [('wre', s), 'tabs', 'ctmp0'], [('car', j)],
                        scale=c512[:, j:j + 1], bias=ctmp[:, 0:1])
                    ACT(ctmp[:, 1:2], wim[s][:, 511:512], AF.Identity, [('wim', s), 'tabs'], ['ctmp1'], scale=c512[:, j:j + 1])
                    ACT(car[:, 1, j:j + 1], wre[s][:, 511:512], AF.Identity, [('wre', s), 'tabs', 'ctmp1'], [('car', j)],
                        scale=s512[:, j:j + 1], bias=ctmp[:, 1:2])

                def emit_rot(j):
                    s = j % 2
                    cs = cosT[:, j, :]
                    sn = sinT[:, j, :]
                    o4 = [ot_[s * 4 + k] for k in range(4)]
                    ok4 = [('ot', s * 4 + k) for k in range(4)]
                    TT('pool', o4[0][:], wre[s][:], cs, ALU.mult, [('wre', s), 'tabs'], [ok4[0]])
                    TT('pool', o4[1][:], wim[s][:], sn, ALU.mult, [('wim', s), 'tabs'], [ok4[1]])
                    TT('pool', o4[2][:], wre[s][:], sn, ALU.mult, [('wre', s), 'tabs'], [ok4[2]])
                    TT('pool', o4[3][:], wim[s][:], cs, ALU.mult, [('wim', s), 'tabs'], [ok4[3]])

                def emit_y(j):
                    c = j // 4
                    s = j % 2
                    pyc = py[c % 2]
                    pyk = ('py', c % 2)
                    o4 = [ot_[s * 4 + k] for k in range(4)]
                    ok4 = [('ot', s * 4 + k) for k in range(4)]
                    MM(pyc[:], CTt[:, j * 3, :], o4[0][:], j % 4 == 0, False, ['CTt', ok4[0]], [pyk])
                    MM(pyc[:], CTt[:, j * 3 + 2, :], o4[1][:], False, False, ['CTt', ok4[1]], [pyk])
                    MM(pyc[:], CTt[:, j * 3 + 1, :], o4[2][:], False, False, ['CTt', ok4[2]], [pyk])
                    MM(pyc[:], CTt[:, j * 3 + 1, :], o4[3][:], False, j % 4 == 3, ['CTt', ok4[3]], [pyk])

                def emit_gelu(c):
                    uk = ('uT', c)
                    pyc = py[c % 2]
                    pyk = ('py', c % 2)
                    STT(yv[:], uT[:, c, :], col("l1_d", c), pyc[:], ALU.mult, ALU.add, [uk, 'colv', pyk], ['zt0'])
                    TT('dve', g1[:], yv[:], yv[:], ALU.mult, ['zt0'], ['zt1'])
                    TS('dve', g1[:], g1[:], 0.044715, 1.0, ALU.mult, ALU.add, ['zt1'], ['zt1'])
                    TT('dve', g1[:], g1[:], yv[:], ALU.mult, ['zt1', 'zt0'], ['zt1'])
                    ACT(g2[:], g1[:], AF.Sigmoid, ['zt1'], ['zt2'], scale=1.5957691216057308)
                    TT('dve', yT[:, c, :], yv[:], g2[:], ALU.mult, ['zt0', 'zt2'], [('yT', c)])

                emit_b(0)
                for j in range(16):
                    emit_zscan(j)
                    if j + 1 < 16:
                        emit_b(j + 1)
                    emit_rot(j)
                    if j >= 1:
                        emit_y(j - 1)
                        if (j - 1) % 4 == 3:
                            emit_gelu((j - 1) // 4)
                    if j % 4 == 2 and j // 4 < 3:
                        emit_u(j // 4 + 1, (j // 4 + 1) % 2)
                emit_y(15)
                emit_gelu(3)
                yr = [('yT', c) for c in range(4)]
                for b in range(4):
                    for hf in range(2):
                        pv = pbr[hf]
                        pgt = pbi[hf]
                        for (pp, ppk, cofs) in ((pv, ('pbr', hf), hf * 512), (pgt, ('pbi', hf), 1024 + hf * 512)):
                            MM(pp[:], ones_r[0:1, :], bglu[0:1, cofs:cofs + 512], True, False, ['ones_r', 'bglu'], [ppk])
                            for c in range(4):
                                MM(pp[:], yT[:, c, b * 128:(b + 1) * 128], w_glu[:, c, cofs:cofs + 512], False, c == 3, [yr[c], 'w_glu'], [ppk])
                        ACT(sg[:], pgt[:], AF.Sigmoid, [('pbi', hf)], ['zt3'])
                        TT('dve', sg[:], pv[:], sg[:], ALU.mult, [('pbr', hf), 'zt3'], ['zt3'])
                        TT('dve', ht[:, b, hf * 512:(hf + 1) * 512], sg[:], ht[:, b, hf * 512:(hf + 1) * 512], ALU.add, ['zt3', hk], [hk])
                DMA(h1m_scr[tsl, :].rearrange("(b p) d -> p b d", p=128), ht[:], [hk], [('h1m', i)], ('hst', i % 2))
        P.barrier()

        with contextlib.ExitStack() as st:
            htb = [sbt(st, "ht%d" % i, [128, 4, 1024], F32) for i in range(3)]
            xsb = [sbt(st, "xs%d" % i, [128, 4, 1024], BF16) for i in range(2)]
            xnTb = [sbt(st, "xnT%d" % i, [128, 8, 512], BF16) for i in range(2)]
            ssqb = [sbt(st, "ssq%d" % i, [128, 4], F32) for i in range(2)]
            rstdb = [sbt(st, "rstd%d" % i, [128, 4], F32) for i in range(2)]
            ssqf = sbt(st, "ssqf", [128, 4], F32)
            rstdf = sbt(st, "rstdf", [128, 4], F32)
            junk = sbt(st, "junkf", [128, 1024], BF16)
            tpb = [pst(st, "tpa%d" % i, [128, 1024], BF16) for i in range(2)]
            fb = ffn_bufs(st)

            def load_b3(i):
                DMA(htb[i % 3][:], h1m_scr[i * 512:(i + 1) * 512, :].rearrange("(b p) d -> p b d", p=128), [('h1m', i)], [('ht', i % 3)], ('ht', i % 3))

            def pre_b3(i):
                return norm_stages(htb[i % 3], ('ht', i % 3), "l1_ffn_norm", xsb[i % 2], xnTb[i % 2], ('xnT', i % 2), tpb,
                                   ssqb[i % 2], rstdb[i % 2], par=i % 2)
            load_b3(0)
            if NT > 1:
                load_b3(1)
            for f in pre_b3(0):
                f()
            for i in range(NT):
                if i + 2 < NT:
                    load_b3(i + 2)
                tsl = slice(i * 512, (i + 1) * 512)
                ht = htb[i % 3]
                hk = ('ht', i % 3)
                hook = None
                if i + 1 < NT:
                    stg = pre_b3(i + 1)
                    hook = (lambda k, stg=stg: stg[k - 8]() if 8 <= k < 8 + len(stg) else None)
                ffn(1, xnTb[i % 2], ht, hk, fb, i, xnkey=('xnT', i % 2), hook=hook, last=(i == NT - 1))
                for b in range(4):
                    ACT(junk[:], ht[:, b, :], AF.Square, [hk], ['junkf', 'ssqf'], accum=ssqf[:, b:b + 1])
                TS('dve', rstdf[:], ssqf[:], 1.0 / D, EPS, ALU.mult, ALU.add, ['ssqf'], ['rstdf'])
                ACT(rstdf[:], rstdf[:], AF.Sqrt, ['rstdf'], ['rstdf'])
                RCP(rstdf[:], rstdf[:], ['rstdf'], ['rstdf'])
                for b in range(4):
                    STT(ht[:, b, :], ht[:, b, :], rstdf[:, b:b + 1], gfin[:], ALU.mult, ALU.mult, [hk, 'rstdf', 'gfin'], [hk])
                DMA(out[tsl, :].rearrange("(b p) d -> p b d", p=128), ht[:], [hk], [('out', i)], ('hst', i % 3))
        P.barrier()
        P.emit()
    return nc


def host_consts(S):
    pos = np.arange(S, dtype=np.float32)
    inv = (10000.0 ** (-np.arange(16, dtype=np.float32) / 16.0)).astype(np.float32)
    ang = pos[:, None] * inv[None, :]
    cos, sin = np.cos(ang).astype(np.float32), np.sin(ang).astype(np.float32)
    sc = np.float32(96.0 ** -0.5)
    rq_c = np.concatenate([cos.T, cos.T], 0) * sc
    rq_s = np.concatenate([-sin.T, sin.T], 0) * sc
    return dict(rq_c=np.ascontiguousarray(rq_c, np.float32), rq_s=np.ascontiguousarray(rq_s, np.float32),
                rk_c=np.ascontiguousarray(cos), rk_s=np.ascontiguousarray(sin))


def swap_uq(w_uq):
    w = np.array(w_uq, np.float32).reshape(256, 8, 96).copy()
    sw = w.copy()
    sw[:, :, 64:80] = w[:, :, 80:96]
    sw[:, :, 80:96] = w[:, :, 64:80]
    return np.ascontiguousarray(sw.reshape(256, 768))


_NC_CACHE = {}


def kernel(**inputs):
    x = np.asarray(inputs["x"], np.float32)
    B, S, _ = x.shape
    if S not in _NC_CACHE:
        _NC_CACHE[S] = build(S)
    nc = _NC_CACHE[S]
    consts = host_consts(S)
    base = {name: np.ascontiguousarray(np.asarray(inputs[name], np.float32)) for name, _ in PARAM_SHAPES}
    base["w_uq_sw"] = swap_uq(inputs["l0_w_uq"])
    base.update(consts)
    n = 8
    in_maps = []
    for c in range(n):
        m = dict(base)
        m["x"] = np.ascontiguousarray(x[c % B])
        in_maps.append(m)
    res = run_bass_kernel_spmd(nc, in_maps, core_ids=list(range(n)))
    return np.stack([np.asarray(res.results[b]["out"], np.float32) for b in range(B)], 0)
```

```python
import contextlib
import math
import numpy as np
import concourse.bass as bass
import concourse.mybir as mybir
from concourse.bass_utils import run_bass_kernel_spmd

F32 = mybir.dt.float32
BF16 = mybir.dt.bfloat16
I32 = mybir.dt.int32
AF = mybir.ActivationFunctionType
ALU = mybir.AluOpType

D = 1024
DFF = 2816
EPS = 1e-6
LN_EPS = 1e-5


class Sched:
    ENG = ('pe', 'act', 'dve', 'pool', 'sp')

    def __init__(self, nc):
        self.nc = nc
        self.ops = {e: [] for e in self.ENG}
        self.lastw = {}
        self.readers = {}
        self.known = {e: {} for e in self.ENG}
        self.latest = {}

    def op(self, eng, fn, reads=(), writes=(), dma=None):
        waits = {}

        def need(sig):
            if sig is None:
                return
            k, v = sig
            if eng == 'pe' and k == ('eng', 'pe'):
                return
            if waits.get(k, 0) < v:
                waits[k] = v
        for b in reads:
            need(self.lastw.get(b))
        for b in writes:
            need(self.lastw.get(b))
            for k, v in self.readers.get(b, {}).items():
                need((k, v))
        kn = self.known[eng]
        w = []
        for k, v in waits.items():
            if kn.get(k, 0) >= v:
                continue
            kn[k] = v
            w.append((k, v))
        if dma is not None:
            key = ('dma', dma)
            inc = 16
        else:
            key = ('eng', eng)
            inc = 1
        val = self.latest.get(key, 0) + inc
        self.latest[key] = val
        sig = (key, val)
        self.ops[eng].append((w, fn, key, inc))
        for b in reads:
            r = self.readers.setdefault(b, {})
            if r.get(key, 0) < val:
                r[key] = val
        for b in writes:
            self.lastw[b] = sig
            self.readers[b] = {}
        return sig

    def barrier(self, skip_casts=False):
        for e in self.ENG:
            kn = self.known[e]
            w = []
            for k, v in self.latest.items():
                if skip_casts and k[0] == 'dma' and isinstance(k[1], tuple) and k[1][0] == 'cast':
                    continue
                if kn.get(k, 0) >= v:
                    continue
                kn[k] = v
                w.append((k, v))
            if w:
                self.ops[e].append((w, None, None, 0))

    def emit(self):
        nc = self.nc
        with contextlib.ExitStack() as st:
            sems = {}
            for i, k in enumerate(sorted(self.latest.keys(), key=str)):
                sems[k] = st.enter_context(nc.semaphore("s%d" % i))
            block = st.enter_context(nc.Block())

            def run(name, e):
                for (w, fn, key, inc) in self.ops[name]:
                    for (k, v) in w:
                        e.wait_ge(sems[k], v)
                    if fn is not None:
                        fn(e).then_inc(sems[key], inc)

            @block.tensor
            def _(e):
                run('pe', e)

            @block.scalar
            def _(e):
                run('act', e)

            @block.vector
            def _(e):
                run('dve', e)

            @block.gpsimd
            def _(e):
                run('pool', e)

            @block.sync
            def _(e):
                run('sp', e)


PARAM_SHAPES = [
    ("l0_mix_norm", [1024]), ("l0_w_in", [1024, 1440]), ("l0_conv_w", [31, 512]), ("l0_conv_b", [512]),
    ("l0_conv_ln_g", [512]), ("l0_conv_ln_b", [512]), ("l0_q_norm", [256]), ("l0_kv_norm", [128]),
    ("l0_w_uq", [256, 768]), ("l0_w_ukv", [128, 1024]), ("l0_w_out", [1024, 1024]), ("l0_ffn_norm", [1024]),
    ("l0_w_up", [1024, 5632]), ("l0_ffn_conv_w", [3, 5632]), ("l0_ffn_conv_b", [5632]), ("l0_w_down", [2816, 1024]),
    ("l1_mix_norm", [1024]), ("l1_w_in", [1024, 512]), ("l1_log_dt", [32]), ("l1_a_re", [32, 64]),
    ("l1_a_im", [32, 64]), ("l1_b_re", [32, 64, 16]), ("l1_b_im", [32, 64, 16]), ("l1_c_re", [32, 16, 64]),
    ("l1_c_im", [32, 16, 64]), ("l1_d", [512]), ("l1_w_glu", [512, 2048]), ("l1_b_glu", [2048]),
    ("l1_ffn_norm", [1024]), ("l1_w_up", [1024, 5632]), ("l1_ffn_conv_w", [3, 5632]), ("l1_ffn_conv_b", [5632]),
    ("l1_w_down", [2816, 1024]), ("final_norm", [1024]),
]


def build(S, dbg=False):
    NT = S // 512
    NB = S // 128
    nc = bass.Bass("TRN2", target_bir_lowering=False)
    I = {}
    I["x"] = nc.dram_tensor("x", [S, D], F32, kind="ExternalInput").ap()
    for name, shp in PARAM_SHAPES:
        I[name] = nc.dram_tensor(name, shp, F32, kind="ExternalInput").ap()
    I["w_uq_sw"] = nc.dram_tensor("w_uq_sw", [256, 768], F32, kind="ExternalInput").ap()
    I["rq_c"] = nc.dram_tensor("rq_c", [32, S], F32, kind="ExternalInput").ap()
    I["rq_s"] = nc.dram_tensor("rq_s", [32, S], F32, kind="ExternalInput").ap()
    I["rk_c"] = nc.dram_tensor("rk_c", [S, 16], F32, kind="ExternalInput").ap()
    I["rk_s"] = nc.dram_tensor("rk_s", [S, 16], F32, kind="ExternalInput").ap()
    out = nc.dram_tensor("out", [S, D], F32, kind="ExternalOutput").ap()

    skind = dict(kind="ExternalOutput") if dbg else {}

    def scr(name, shp, dt, **kw):
        return nc.dram_tensor(name, shp, dt, **kw).ap()
    WB = {}
    for name in ["l0_w_in", "l0_w_uq", "w_uq_sw", "l0_w_ukv", "l0_w_out", "l1_w_in", "l1_w_glu"]:
        WB[name] = scr("bf_" + name, list(I[name].shape), BF16)
    for L_ in range(2):
        WB["l%d_w_up" % L_] = scr("bf_l%d_w_up" % L_, [11, 128, 4096], BF16)
        WB["l%d_w_down" % L_] = scr("bf_l%d_w_down" % L_, [4, 128, 22 * 256], BF16)
    hp_scr = scr("hp_scr", [S, D], F32, **skind)
    h1_scr = scr("h1_scr", [S, D], F32, **skind)
    h1m_scr = scr("h1m_scr", [S, D], F32, **skind)
    att_scr = scr("att_scr", [512, S], BF16)

    P = Sched(nc)

    def MM(o, lhsT, rhs, start, stop, reads, writes):
        P.op('pe', lambda e: e.matmul(o, lhsT=lhsT, rhs=rhs, start=start, stop=stop), reads, writes)

    def TR(o, in_, ident, reads, writes):
        P.op('pe', lambda e: e.transpose(o, in_, ident), reads, writes)

    def ACT(o, in_, func, reads, writes, scale=None, bias=None, accum=None):
        kw = {}
        if scale is not None:
            kw['scale'] = scale
        if bias is not None:
            kw['bias'] = bias
        if accum is not None:
            kw['accum_out'] = accum
        P.op('act', lambda e: e.activation(out=o, in_=in_, func=func, **kw), reads, writes)

    def TS(eng, o, in0, s1, s2, op0, op1, reads, writes):
        if op1 is None:
            P.op(eng, lambda e: e.tensor_scalar(out=o, in0=in0, scalar1=s1, scalar2=None, op0=op0), reads, writes)
        else:
            P.op(eng, lambda e: e.tensor_scalar(out=o, in0=in0, scalar1=s1, scalar2=s2, op0=op0, op1=op1), reads, writes)

    def STT(o, in0, scalar, in1, op0, op1, reads, writes):
        P.op('dve', lambda e: e.scalar_tensor_tensor(out=o, in0=in0, scalar=scalar, in1=in1, op0=op0, op1=op1), reads, writes)

    def TT(eng, o, in0, in1, op, reads, writes):
        P.op(eng, lambda e: e.tensor_tensor(out=o, in0=in0, in1=in1, op=op), reads, writes)

    def CP(eng, o, in_, reads, writes):
        P.op(eng, lambda e: e.tensor_copy(out=o, in_=in_), reads, writes)

    def RCP(o, in_, reads, writes):
        P.op('dve', lambda e: e.reciprocal(out=o, in_=in_), reads, writes)

    def MS(eng, ap, val, writes):
        P.op(eng, lambda e: e.memset(ap, val), (), writes)

    def DMA(o, in_, reads, writes, key, eng='sp', slow=False, maxlast=None):
        kw = {}
        if slow:
            kw['allow_slow_non_contiguous'] = True
        if maxlast is not None:
            kw['max_dma_last_dim'] = maxlast
        P.op(eng, lambda e: e.dma_start(out=o, in_=in_, **kw), reads, writes, dma=key)

    def SCAN(o, d0, d1, init, reads, writes):
        P.op('dve', lambda e: e.tensor_tensor_scan(out=o, data0=d0, data1=d1, initial=init, op0=ALU.mult, op1=ALU.add), reads, writes)

    with contextlib.ExitStack() as top:
        uid = [0]

        def sbt(st, name, shape, dt):
            uid[0] += 1
            return st.enter_context(nc.sbuf_tensor("%s_%d" % (name, uid[0]), shape, dt))

        def pst(st, name, shape, dt):
            uid[0] += 1
            return st.enter_context(nc.psum_tensor("%s_%d" % (name, uid[0]), shape, dt))

        ident = sbt(top, "ident", [128, 128], BF16)
        identf = sbt(top, "identf", [128, 128], F32)
        colv = sbt(top, "colv", [128, 640], F32)
        gfin = sbt(top, "gfin", [128, 1024], F32)
        eps_c = sbt(top, "eps_c", [128, 1], F32)
        MS('pool', eps_c[:], 0.0, ['eps_c'])

        MS('pool', identf[:], 1.0, ['identf'])
        P.op('pool', lambda e: e.affine_select(out=identf[:], in_=identf[:], pattern=[[-1, 128]], compare_op=ALU.is_equal,
                                               fill=0.0, base=0, channel_multiplier=1), ['identf'], ['identf'])
        CP('dve', ident[:], identf[:], ['identf'], ['ident'])
        def cast_plain(name):
            rows = I[name].shape[0]
            for r0 in range(0, rows, 128):
                DMA(WB[name][r0:r0 + 128, :], I[name][r0:r0 + 128, :], (), [('wb', name)], ('cast', name), eng='pool', maxlast=4096)

        def cast_ffn(L_):
            nu = "l%d_w_up" % L_
            nd = "l%d_w_down" % L_
            for m in range(11):
                for half in range(2):
                    c0 = half * DFF + m * 256
                    src = I[nu][:, c0:c0 + 256].rearrange("(k p) c -> p k c", p=128)
                    dst = WB[nu][m].rearrange("p (k c) -> p k c", c=512)[:, :, half * 256:(half + 1) * 256]
                    DMA(dst, src, (), [('wb', nu)], ('cast', nu), eng='pool', maxlast=4096)
            for q in range(4):
                src = I[nd][:, q * 256:(q + 1) * 256].rearrange("(k p) c -> p k c", p=128)
                dst = WB[nd][q].rearrange("p (k c) -> p k c", c=256)
                DMA(dst, src, (), [('wb', nd)], ('cast', nd), eng='pool', maxlast=4096)
        for name in ["l0_w_in", "l0_w_out", "l0_w_uq", "w_uq_sw", "l0_w_ukv"]:
            cast_plain(name)
        cast_ffn(0)
        cast_plain("l1_w_in")
        cast_plain("l1_w_glu")
        cast_ffn(1)
        DMA(gfin[:], I["final_norm"].partition_broadcast(128), (), ['gfin'], 'gfin')

        VEC = [("l0_mix_norm", 8), ("l0_ffn_norm", 8), ("l1_mix_norm", 8), ("l1_ffn_norm", 8),
               ("l0_conv_b", 4), ("l0_conv_ln_g", 4), ("l0_conv_ln_b", 4), ("l0_q_norm", 2), ("l0_kv_norm", 1),
               ("l1_d", 4), ("l1_a_re", 16), ("l1_a_im", 16),
               ("l0_conv_w", 124), ("l0_ffn_conv_w", 132), ("l0_ffn_conv_b", 44),
               ("l1_ffn_conv_w", 132), ("l1_ffn_conv_b", 44)]
        COL = {}
        r = 0
        for name, n in VEC:
            COL[name] = r
            r += n
        NST = (r + 127) // 128
        assert NST * 128 <= 640
        with contextlib.ExitStack() as st:
            stage = sbt(st, "stage", [128, NST, 128], F32)
            pstg = pst(st, "pstg", [128, 512], F32)
            MS('dve', stage[:], 0.0, ['stage'])
            stoks = []
            for name, n in VEC:
                flat = I[name]
                if len(flat.shape) == 2:
                    flat = flat.rearrange("a b -> (a b)")
                src = flat.rearrange("(r c) -> r c", c=128)
                done = 0
                while done < n:
                    r0 = COL[name] + done
                    cnt = min(n - done, 128 - (r0 % 128))
                    tok = ('stg', name, done)
                    stoks.append(tok)
                    DMA(stage[r0 % 128:r0 % 128 + cnt, r0 // 128, :], src[done:done + cnt, :], ['stage'], [tok], 'stage')
                    done += cnt
            for t in range(NST):
                TR(pstg[:, 0:128], stage[:, t, :], identf[:], stoks + ['stage', 'identf'], ['pstg'])
                CP('dve', colv[:, t * 128:(t + 1) * 128], pstg[:, 0:128], ['pstg'], ['colv'])
        P.barrier(skip_casts=True)

        def col(name, i=0, n=1):
            c = COL[name] + i
            return colv[:, c:c + n]

        def norm_stages(h, hkey, gname, xs, xnT, xnkey, tp, ssq, rstd, par=0):
            xsk = ('xs', par)
            sk = ('ssq', par)
            rk = ('rstd', par)

            def s_sq():
                for b in range(4):
                    ACT(xs[:, b, :], h[:, b, :], AF.Square, [hkey], [xsk, sk], accum=ssq[:, b:b + 1])

            def s_rstd():
                TS('dve', rstd[:], ssq[:], 1.0 / D, EPS, ALU.mult, ALU.add, [sk], [rk])
                ACT(rstd[:], rstd[:], AF.Sqrt, [rk], [rk])
                RCP(rstd[:], rstd[:], [rk], [rk])

            def s_scale():
                for b in range(4):
                    TS('dve', xs[:, b, :], h[:, b, :], rstd[:, b:b + 1], None, ALU.mult, None, [hkey, rk], [xsk])

            def s_tr(k0, k1):
                for kc in range(k0, k1):
                    t = tp[kc % 2]
                    tk = ('tp', kc % 2)
                    for b in range(4):
                        TR(t[:, b * 128:(b + 1) * 128], xs[:, b, kc * 128:(kc + 1) * 128], ident[:], [xsk, 'ident'], [tk])
                    ACT(xnT[:, kc, :], t[:, 0:512], AF.Copy, [tk, 'colv'], [(xnkey, kc)], scale=col(gname, kc))
            return [s_sq, s_rstd, s_scale] + [(lambda k=k: s_tr(k, k + 1)) for k in range(8)]

        def norm_T(st_name, h, hkey, gname, xs, xnT, xnkey, tp, ssq, rstd, par=0):
            for f in norm_stages(h, hkey, gname, xs, xnT, xnkey, tp, ssq, rstd, par):
                f()

        with contextlib.ExitStack() as st:
            xt = [sbt(st, "xt%d" % i, [128, 4, 1024], F32) for i in range(3)]
            xsb = [sbt(st, "xs%d" % i, [128, 4, 1024], BF16) for i in range(2)]
            xnTb = [sbt(st, "xnT%d" % i, [128, 8, 512], BF16) for i in range(2)]
            ssqb = [sbt(st, "ssq%d" % i, [128, 4], F32) for i in range(2)]
            rstdb = [sbt(st, "rstd%d" % i, [128, 4], F32) for i in range(2)]
            w_in = sbt(st, "w_in", [128, 8, 1024], BF16)
            w_o = sbt(st, "w_o", [128, 4, 1024], BF16)
            diag = sbt(st, "diag", [128, 124, 128], BF16)
            ubuf = sbt(st, "ubuf", [128, 4, 544], BF16)
            sig = [sbt(st, "sig%d" % i, [128, 512], BF16) for i in range(2)]
            ycv = sbt(st, "ycv", [128, 4, 512], F32)
            ysq = sbt(st, "ysq", [128, 4, 512], F32)
            mean = sbt(st, "mean", [128, 512], F32)
            var = sbt(st, "var", [128, 512], F32)
            dcv = [sbt(st, "dcv%d" % i, [128, 512], F32) for i in range(2)]
            cvT = sbt(st, "cvT", [128, 4, 512], BF16)
            onesf = sbt(st, "onesf", [128, 128], F32)
            tpb = [pst(st, "tpa%d" % i, [128, 1024], BF16) for i in range(2)]
            bk = [pst(st, "bk%d" % i, [128, 512], F32) for i in range(6)]

            MS('dve', onesf[:], 1.0 / 512.0, ['onesf'])
            MS('dve', ubuf[:], 0.0, [('ub', c_) for c_ in range(4)])
            DMA(w_in[:], WB["l0_w_in"][:, 0:1024].rearrange("(k p) c -> p k c", p=128), [('wb', "l0_w_in")], ['w_in'], 'w_in')
            DMA(w_o[:], WB["l0_w_out"][0:512, :].rearrange("(k p) c -> p k c", p=128), [('wb', "l0_w_out")], ['w_o'], 'w_o')
            for kk in range(31):
                for c in range(4):
                    TS('dve', diag[:, kk * 4 + c, :], identf[:], col("l0_conv_w", kk * 4 + c), None, ALU.mult, None,
                       ['identf', 'colv'], ['diag'])

            def load_x(i):
                DMA(xt[i % 3][:], I["x"][i * 512:(i + 1) * 512, :].rearrange("(b p) d -> p b d", p=128), (), [('xt', i % 3)], ('xt', i % 3))
            def pre_a1(i):
                return norm_stages(xt[i % 3], ('xt', i % 3), "l0_mix_norm", xsb[i % 2], xnTb[i % 2], ('xnT', i % 2), tpb,
                                   ssqb[i % 2], rstdb[i % 2], par=i % 2)
            load_x(0)
            if NT > 1:
                load_x(1)
            for f in pre_a1(0):
                f()
            for i in range(NT):
                if i + 2 < NT:
                    load_x(i + 2)
                h = xt[i % 3]
                hk = ('xt', i % 3)
                xnT = xnTb[i % 2]
                xr = [(('xnT', i % 2), kc) for kc in range(8)]
                stg = pre_a1(i + 1) if i + 1 < NT else []
                stg_it = iter(stg)

                def unit(n=1):
                    for _ in range(n):
                        f = next(stg_it, None)
                        if f is not None:
                            f()
                if i > 0:
                    for c in range(4):
                        CP('dve', ubuf[:, c, 0:30], ubuf[:, c, 512:542], [('ub', c)], [('ub', c)])
                for c in range(4):
                    pa, pak = bk[(c % 2) * 2], ('bk', (c % 2) * 2)
                    pg, pgk = bk[(c % 2) * 2 + 1], ('bk', (c % 2) * 2 + 1)
                    for kc in range(8):
                        MM(pa[:], w_in[:, kc, c * 128:(c + 1) * 128], xnT[:, kc, :], kc == 0, kc == 7, ['w_in', xr[kc]], [pak])
                    for kc in range(8):
                        MM(pg[:], w_in[:, kc, 512 + c * 128:512 + (c + 1) * 128], xnT[:, kc, :], kc == 0, kc == 7, ['w_in', xr[kc]], [pgk])
                    ACT(sig[c % 2][:], pg[:], AF.Sigmoid, [pgk], [('sig', c % 2)])
                    TT('dve', ubuf[:, c, 30:542], pa[:], sig[c % 2][:], ALU.mult, [pak, ('sig', c % 2)], [('ub', c)])
                    if c >= 1:
                        unit()
                for c in range(4):
                    pcv, pck = bk[4 + c % 2], ('bk', 4 + c % 2)
                    for kk in range(31):
                        MM(pcv[:], diag[:, kk * 4 + c, :], ubuf[:, c, kk:kk + 512], kk == 0, kk == 30, ['diag', ('ub', c)], [pck])
                    ACT(ycv[:, c, :], pcv[:], AF.Identity, [pck, 'colv'], [('ycv', c)], bias=col("l0_conv_b", c))
                    ACT(ysq[:, c, :], ycv[:, c, :], AF.Square, [('ycv', c)], [('ysq', c)])
                    unit()
                pmean, pmk = bk[0], ('bk', 0)
                pmsq, pqk = bk[1], ('bk', 1)
                for c in range(4):
                    MM(pmean[:], onesf[:], ycv[:, c, :], c == 0, c == 3, ['onesf', ('ycv', c)], [pmk])
                for c in range(4):
                    MM(pmsq[:], onesf[:], ysq[:, c, :], c == 0, c == 3, ['onesf', ('ysq', c)], [pqk])
                CP('dve', mean[:], pmean[:], [pmk], ['mean'])
                TT('dve', var[:], mean[:], mean[:], ALU.mult, ['mean'], ['var'])
                TT('dve', var[:], pmsq[:], var[:], ALU.subtract, [pqk, 'var'], ['var'])
                TS('dve', var[:], var[:], LN_EPS, None, ALU.add, None, ['var'], ['var'])
                ACT(var[:], var[:], AF.Sqrt, ['var'], ['var'])
                RCP(var[:], var[:], ['var'], ['var'])
                for c in range(4):
                    TT('dve', dcv[c % 2][:], ycv[:, c, :], mean[:], ALU.subtract, [('ycv', c), 'mean'], [('dcv', c % 2)])
                    TT('dve', dcv[c % 2][:], dcv[c % 2][:], var[:], ALU.mult, [('dcv', c % 2), 'var'], [('dcv', c % 2)])
                    ACT(cvT[:, c, :], dcv[c % 2][:], AF.Silu, [('dcv', c % 2), 'colv'], [('cvT', c)], scale=col("l0_conv_ln_g", c), bias=col("l0_conv_ln_b", c))
                n_ = 0
                for b in range(4):
                    for hf in range(2):
                        po, pok = bk[2 + n_ % 4], ('bk', 2 + n_ % 4)
                        n_ += 1
                        for c in range(4):
                            MM(po[:], cvT[:, c, b * 128:(b + 1) * 128], w_o[:, c, hf * 512:(hf + 1) * 512], c == 0, c == 3,
                               [('cvT', c), 'w_o'], [pok])
                        TT('dve', h[:, b, hf * 512:(hf + 1) * 512], po[:], h[:, b, hf * 512:(hf + 1) * 512], ALU.add, [pok, hk], [hk])
                        unit()
                DMA(hp_scr[i * 512:(i + 1) * 512, :].rearrange("(b p) d -> p b d", p=128), h[:], [hk], [('hp', i)], ('xst', i % 3))
        P.barrier(skip_casts=True)

        with contextlib.ExitStack() as st:
            cqnT = sbt(st, "cqnT", [128, 2, S], BF16)
            ckvnT = sbt(st, "ckvnT", [128, S], BF16)
            KT = sbt(st, "KT", [128, S], BF16)
            with contextlib.ExitStack() as s2:
                xt = [sbt(s2, "xt%d" % i, [128, 4, 1024], F32) for i in range(3)]
                xsb = [sbt(s2, "xs%d" % i, [128, 4, 1024], BF16) for i in range(2)]
                xnTb = [sbt(s2, "xnT%d" % i, [128, 8, 512], BF16) for i in range(2)]
                ssqb = [sbt(s2, "ssq%d" % i, [128, 4], F32) for i in range(2)]
                rstdb = [sbt(s2, "rstd%d" % i, [128, 4], F32) for i in range(2)]
                w_sm = sbt(s2, "w_sm", [128, 8, 416], BF16)
                rkc = sbt(s2, "rkc", [128, NB, 16], F32)
                rks = sbt(s2, "rks", [128, NB, 16], F32)
                junk = sbt(s2, "junk", [128, 256], BF16)
                ss2 = sbt(s2, "ss2", [128, 2], F32)
                cqs = sbt(s2, "cqs", [128, 4, 256], BF16)
                ckvs = sbt(s2, "ckvs", [128, 4, 128], BF16)
                kst = sbt(s2, "kst", [128, 4, 96], BF16)
                tmp = sbt(s2, "tmp", [128, 4, 16], F32)
                tpb = [pst(s2, "tpa%d" % i, [128, 1024], BF16) for i in range(2)]
                psm = [pst(s2, "psm%d" % i, [128, 512], F32) for i in range(2)]
                tq = [pst(s2, "tq%d" % i, [128, 1024], BF16) for i in range(4)]

                DMA(w_sm[:], WB["l0_w_in"][:, 1024:1440].rearrange("(k p) c -> p k c", p=128), [('wb', "l0_w_in")], ['w_sm'], 'w_sm')
                DMA(rkc[:], I["rk_c"].rearrange("(b p) d -> p b d", p=128), (), ['rkc'], 'rkc')
                DMA(rks[:], I["rk_s"].rearrange("(b p) d -> p b d", p=128), (), ['rks'], 'rks')
                MS('dve', kst[:], 0.0, ['kst'])

                def load_x2(i):
                    DMA(xt[i % 3][:], I["x"][i * 512:(i + 1) * 512, :].rearrange("(b p) d -> p b d", p=128), (), [('xt', i % 3)], ('xt', i % 3))
                def pre_a2(i):
                    return norm_stages(xt[i % 3], ('xt', i % 3), "l0_mix_norm", xsb[i % 2], xnTb[i % 2], ('xnT', i % 2), tpb,
                                       ssqb[i % 2], rstdb[i % 2], par=i % 2)
                load_x2(0)
                if NT > 1:
                    load_x2(1)
                for f in pre_a2(0):
                    f()
                for i in range(NT):
                    if i + 2 < NT:
                        load_x2(i + 2)
                    h = xt[i % 3]
                    hk = ('xt', i % 3)
                    xnT = xnTb[i % 2]
                    xr = [(('xnT', i % 2), kc) for kc in range(8)]
                    stg_it = iter(pre_a2(i + 1) if i + 1 < NT else [])

                    def unit(n=1):
                        for _ in range(n):
                            f = next(stg_it, None)
                            if f is not None:
                                f()
                    for b in range(4):
                        pp = psm[b % 2]
                        pk = ('psm', b % 2)
                        gb = i * 4 + b
                        for kc in range(8):
                            MM(pp[:, 0:416], xnT[:, kc, b * 128:(b + 1) * 128], w_sm[:, kc, :], kc == 0, kc == 7, [xr[kc], 'w_sm'], [pk])
                        ACT(junk[:, 0:256], pp[:, 0:256], AF.Square, [pk], ['junk', 'ss2'], accum=ss2[:, 0:1])
                        ACT(junk[:, 0:128], pp[:, 256:384], AF.Square, [pk], ['junk', 'ss2'], accum=ss2[:, 1:2])
                        TS('dve', ss2[:, 0:1], ss2[:, 0:1], 1.0 / 256, EPS, ALU.mult, ALU.add, ['ss2'], ['ss2'])
                        TS('dve', ss2[:, 1:2], ss2[:, 1:2], 1.0 / 128, EPS, ALU.mult, ALU.add, ['ss2'], ['ss2'])
                        ACT(ss2[:], ss2[:], AF.Sqrt, ['ss2'], ['ss2'])
                        RCP(ss2[:], ss2[:], ['ss2'], ['ss2'])
                        TS('dve', cqs[:, b, :], pp[:, 0:256], ss2[:, 0:1], None, ALU.mult, None, [pk, 'ss2'], ['cqs'])
                        TS('dve', ckvs[:, b, :], pp[:, 256:384], ss2[:, 1:2], None, ALU.mult, None, [pk, 'ss2'], ['ckvs'])
                        TT('dve', tmp[:, 0, :], pp[:, 384:400], rkc[:, gb, :], ALU.mult, [pk, 'rkc'], ['tmp'])
                        TT('dve', tmp[:, 1, :], pp[:, 400:416], rks[:, gb, :], ALU.mult, [pk, 'rks'], ['tmp'])
                        TT('dve', tmp[:, 2, :], pp[:, 384:400], rks[:, gb, :], ALU.mult, [pk, 'rks'], ['tmp'])
                        TT('dve', tmp[:, 3, :], pp[:, 400:416], rkc[:, gb, :], ALU.mult, [pk, 'rkc'], ['tmp'])
                        TT('dve', kst[:, b, 64:80], tmp[:, 0, :], tmp[:, 1, :], ALU.subtract, ['tmp'], ['kst'])
                        TT('dve', kst[:, b, 80:96], tmp[:, 2, :], tmp[:, 3, :], ALU.add, ['tmp'], ['kst'])
                        unit(2)
                    tsl = slice(i * 512, (i + 1) * 512)
                    for b in range(4):
                        TR(tq[0][:, b * 128:(b + 1) * 128], cqs[:, b, 0:128], ident[:], ['cqs', 'ident'], [('tq', 0)])
                        TR(tq[1][:, b * 128:(b + 1) * 128], cqs[:, b, 128:256], ident[:], ['cqs', 'ident'], [('tq', 1)])
                        TR(tq[2][:, b * 128:(b + 1) * 128], ckvs[:, b, :], ident[:], ['ckvs', 'ident'], [('tq', 2)])
                        TR(tq[3][0:96, b * 128:(b + 1) * 128], kst[:, b, :], ident[:], ['kst', 'ident'], [('tq', 3)])
                    ACT(cqnT[:, 0, tsl], tq[0][:, 0:512], AF.Copy, [('tq', 0), 'colv'], ['cqnT'], scale=col("l0_q_norm", 0))
                    ACT(cqnT[:, 1, tsl], tq[1][:, 0:512], AF.Copy, [('tq', 1), 'colv'], ['cqnT'], scale=col("l0_q_norm", 1))
                    ACT(ckvnT[:, tsl], tq[2][:, 0:512], AF.Copy, [('tq', 2), 'colv'], ['ckvnT'], scale=col("l0_kv_norm", 0))
                    CP('dve', KT[64:96, tsl], tq[3][64:96, 0:512], [('tq', 3)], ['KTr'])
                    unit(4)
            P.barrier(skip_casts=True)

            with contextlib.ExitStack() as s2:
                rqc = sbt(s2, "rqc", [128, S], BF16)
                rqs = sbt(s2, "rqs", [128, S], BF16)
                w_uq = sbt(s2, "w_uq", [128, 2, 768], BF16)
                w_uqs = sbt(s2, "w_uqs", [128, 2, 768], BF16)
                w_ukv = sbt(s2, "w_ukv", [128, 1024], BF16)
                Vb = [sbt(s2, "Vb%d" % i, [128, NB, 128], BF16) for i in range(2)]
                QT = [sbt(s2, "QT%d" % i, [128, 512], BF16) for i in range(2)]
                PT = [sbt(s2, "PT%d" % i, [128, 512], BF16) for i in range(4)]
                tri = sbt(s2, "tri", [128, 128], BF16)
                trif = sbt(s2, "trif", [128, 128], F32)
                osb = [sbt(s2, "osb%d" % i, [128, 512], F32) for i in range(2)]
                rinv = sbt(s2, "rinv", [128, 512], F32)
                aout = [sbt(s2, "aout%d" % i, [128, 512], BF16) for i in range(2)]
                qtmp = sbt(s2, "qtmp", [128, 2, 512], F32)
                sel = [sbt(s2, "sel%d" % i, [128, 128], F32) for i in range(2)]
                pq = pst(s2, "pq", [128, 512], F32)
                pqs = pst(s2, "pqs", [128, 512], F32)
                pss = [pst(s2, "pss%d" % i, [128, 512], F32) for i in range(3)]
                pov = [pst(s2, "pov%d" % i, [128, 512], F32) for i in range(2)]
                pkv = pst(s2, "pkv", [128, 512], F32)
                prs = pkv

                DMA(rqc[64:96, :], I["rq_c"], (), ['rqc'], 'rqc', eng='pool', maxlast=4096)
                DMA(rqs[64:96, :], I["rq_s"], (), ['rqs'], 'rqs', eng='pool', maxlast=4096)
                DMA(w_uq[:], WB["l0_w_uq"].rearrange("(k p) c -> p k c", p=128), [('wb', "l0_w_uq")], ['w_uq'], 'w_uq')
                DMA(w_uqs[:], WB["w_uq_sw"].rearrange("(k p) c -> p k c", p=128), [('wb', "w_uq_sw")], ['w_uqs'], 'w_uqs')
                DMA(w_ukv[:], WB["l0_w_ukv"], [('wb', "l0_w_ukv")], ['w_ukv'], 'w_ukv')
                MS('pool', trif[:], 1.0, ['trif'])
                P.op('pool', lambda e: e.affine_select(out=trif[:], in_=trif[:], pattern=[[1, 128]], compare_op=ALU.is_ge,
                                                       fill=0.0, base=0, channel_multiplier=-1), ['trif'], ['trif'])
                CP('dve', tri[:], trif[:], ['trif'], ['tri'])
                MS('pool', sel[0][:], 0.0, ['sel0'])
                MS('pool', sel[1][:], 0.0, ['sel1'])
                MS('pool', sel[0][64:65, 0:64], 1.0, ['sel0'])
                MS('pool', sel[1][0:1, 64:128], 1.0, ['sel1'])
                MS('pool', Vb[0][:], 0.0, [('V', 0)])
                MS('pool', Vb[1][:], 0.0, [('V', 1)])
                MS('pool', Vb[0][:, :, 64:65], 1.0, [('V', 0)])
                MS('pool', Vb[1][:, :, 0:1], 1.0, [('V', 1)])
                qscale = 96.0 ** -0.5

                for hd in range(8):
                    par = hd % 2
                    V = Vb[par]
                    vk = ('V', par)
                    rlo = 64 * par
                    xb = [(pkv, 'pkv'), (pss[0], ('pss', 0)), (pss[1], ('pss', 1)), (pss[2], ('pss', 2))]
                    for kt in range(NT):
                        ksl = slice(kt * 512, (kt + 1) * 512)
                        pb, pbk = xb[kt % 4]
                        MM(pb[0:64, :], w_ukv[:, hd * 128:hd * 128 + 64], ckvnT[:, ksl], True, True, ['w_ukv', 'ckvnT'], [pbk])
                        ACT(KT[0:64, ksl], pb[0:64, :], AF.Copy, [pbk], ['KTn'])
                    for kg in range(NB // 8):
                        pb, pbk = xb[kg % 4]
                        for j in range(8):
                            kb = kg * 8 + j
                            MM(pb[:, j * 64:(j + 1) * 64], ckvnT[:, kb * 128:(kb + 1) * 128], w_ukv[:, hd * 128 + 64:hd * 128 + 128],
                               True, True, ['ckvnT', 'w_ukv'], [pbk])
                        CP('dve', V[:, kg * 8:(kg + 1) * 8, rlo:rlo + 64], pb[:].rearrange("p (j d) -> p j d", d=64), [pbk], [vk])
                    def emit_Q(qt):
                        qsl = slice(qt * 512, (qt + 1) * 512)
                        Q = QT[qt % 2]
                        qk = ('QT', qt % 2)
                        for kc in range(2):
                            MM(pq[0:96, :], w_uq[:, kc, hd * 96:(hd + 1) * 96], cqnT[:, kc, qsl], kc == 0, kc == 1, ['w_uq', 'cqnT'], ['pq'])
                        for kc in range(2):
                            MM(pqs[0:96, :], w_uqs[:, kc, hd * 96:(hd + 1) * 96], cqnT[:, kc, qsl], kc == 0, kc == 1, ['w_uqs', 'cqnT'], ['pqs'])
                        ACT(Q[0:64, :], pq[0:64, :], AF.Copy, ['pq'], [qk], scale=qscale)
                        TT('dve', qtmp[64:96, 0, :], pq[64:96, :], rqc[64:96, qsl], ALU.mult, ['pq', 'rqc'], ['qtmp'])
                        TT('dve', qtmp[64:96, 1, :], pqs[64:96, :], rqs[64:96, qsl], ALU.mult, ['pqs', 'rqs'], ['qtmp'])
                        TT('dve', Q[64:96, :], qtmp[64:96, 0, :], qtmp[64:96, 1, :], ALU.add, ['qtmp'], [qk])

                    def emit_S(qt, kb):
                        Q = QT[qt % 2]
                        qk = ('QT', qt % 2)
                        dg = kb - qt * 4
                        c0 = max(dg, 0) * 128
                        ps_ = pss[kb % 3]
                        psk = ('pss', kb % 3)
                        pt = PT[kb % 4]
                        ptk = ('PT', kb % 4)
                        MM(ps_[:, c0:512], KT[0:96, kb * 128:(kb + 1) * 128], Q[0:96, c0:512], True, True, ['KTn', 'KTr', qk], [psk])
                        ACT(pt[:, c0:512], ps_[:, c0:512], AF.Exp, [psk], [ptk])
                        if dg >= 0:
                            TT('dve', pt[:, c0:c0 + 128], pt[:, c0:c0 + 128], tri[:], ALU.mult, [ptk, 'tri'], [ptk])

                    def emit_PV(qt, kb, nkb):
                        dg = kb - qt * 4
                        c0 = max(dg, 0) * 128
                        MM(pov[qt % 2][:, c0:512], V[:, kb, :], PT[kb % 4][:, c0:512], kb == 0, kb == nkb - 1,
                           [vk, ('PT', kb % 4)], [('pov', qt % 2)])

                    def emit_epi(qt):
                        qsl = slice(qt * 512, (qt + 1) * 512)
                        ob = osb[qt % 2]
                        obk = ('osb', qt % 2)
                        MM(prs[:], sel[par][:], ob[:], True, True, ['sel%d' % par, obk], ['pkv'])
                        RCP(rinv[rlo:rlo + 64, :], prs[rlo:rlo + 64, :], ['pkv'], ['rinv'])
                        ao = aout[qt % 2]
                        aok = ('aout', qt % 2)
                        TT('dve', ao[rlo:rlo + 64, :], ob[rlo:rlo + 64, :], rinv[rlo:rlo + 64, :], ALU.mult, [obk, 'rinv'], [aok])
                        DMA(att_scr[hd * 64:(hd + 1) * 64, qsl], ao[rlo:rlo + 64, :], [aok], [('att', qt)], ('aout', qt % 2))

                    emit_Q(0)
                    pending = None
                    for qt in range(NT):
                        nkb = (qt + 1) * 4
                        if qt + 1 < NT:
                            emit_Q(qt + 1)
                        emit_S(qt, 0)
                        emit_S(qt, 1)
                        emit_S(qt, 2)
                        if pending is not None:
                            emit_epi(pending)
                        for kb in range(nkb):
                            emit_PV(qt, kb, nkb)
                            if kb + 3 < nkb:
                                emit_S(qt, kb + 3)
                        ACT(osb[qt % 2][:], pov[qt % 2][:], AF.Copy, [('pov', qt % 2)], [('osb', qt % 2)])
                        pending = qt
                    emit_epi(pending)
        P.barrier()

        def ffn(L, xnT, h, hk, fb, ti, xnkey='xnT', hook=None, last=False):
            wu, wd, acc, hact, sil, pu, pd = fb
            pre = "l%d_" % L
            wup = WB[pre + "w_up"]
            wdn = WB[pre + "w_down"]
            cw = pre + "ffn_conv_w"
            cb = pre + "ffn_conv_b"
            xr = [(xnkey, kc) for kc in range(8)]
            cur, nxt = ti % 2, (ti + 1) % 2

            def load_wu(m):
                s_ = m % 3
                DMA(wu[s_][:].rearrange("p k c -> p (k c)"), wup[m], [('wb', pre + "w_up")], [('wu', s_)], ('wu', s_))

            def load_wd(q):
                DMA(wd[q][:].rearrange("p k c -> p (k c)"), wdn[q], [('wb', pre + "w_down")], [('wd', q)], ('wd', q))
            if ti == 0:
                load_wu(0)
                load_wu(1)
            hall = [('halo', L, cur, o_) for o_ in range(44)]
            w0c = colv[:, COL[cw]:COL[cw] + 44]
            w1c = colv[:, COL[cw] + 44:COL[cw] + 88]
            hc = halo[:, L, cur]
            TT('dve', corr[:, :, 1], hc[:, :, 1], w0c, ALU.mult, hall + ['colv'], ['corr'])
            TT('dve', ctm[:, :], hc[:, :, 0], w0c, ALU.mult, hall + ['colv'], ['ctm'])
            TT('dve', corr[:, :, 0], hc[:, :, 1], w1c, ALU.mult, hall + ['colv'], ['corr'])
            TT('dve', corr[:, :, 0], corr[:, :, 0], ctm[:, :], ALU.add, ['corr', 'ctm'], ['corr'])
            for m in range(11):
                if m + 2 < 11:
                    load_wu(m + 2)
                if m in (1, 3, 5, 7):
                    load_wd((m - 1) // 2)
                w = wu[m % 3]
                wk = ('wu', m % 3)
                for jj in range(2):
                    j = m * 2 + jj
                    if hook is not None:
                        hook(j)
                    for half in range(2):
                        ot = j + 22 * half
                        bi = jj * 2 + half
                        pp = pu[bi]
                        ppk = ('pu', bi)
                        ac = acc[bi]
                        ack = ('acc', bi)
                        cofs = half * 256 + jj * 128
                        hcur = ('halo', L, cur, ot)
                        hnxt = ('halo', L, nxt, ot)
                        for kc in range(8):
                            MM(pp[:], w[:, kc, cofs:cofs + 128], xnT[:, kc, :], kc == 0, kc == 7, [wk, xr[kc]], [ppk])
                        ACT(ac[:], pp[:], AF.Identity, [ppk, 'colv'], [ack], scale=col(cw, 2 * 44 + ot), bias=col(cb, ot))
                        ACT(halo[:, L, nxt, ot, :], pp[:, 510:512], AF.Copy, [ppk], [hnxt])
                        STT(ac[:, 1:512], pp[:, 0:511], col(cw, 1 * 44 + ot), ac[:, 1:512], ALU.mult, ALU.add, [ppk, 'colv', ack], [ack])
                        STT(ac[:, 2:512], pp[:, 0:510], col(cw, 0 * 44 + ot), ac[:, 2:512], ALU.mult, ALU.add, [ppk, 'colv', ack], [ack])
                        TT('dve', ac[:, 0:2], ac[:, 0:2], corr[:, ot, :], ALU.add, ['corr', ack], [ack])
                    ACT(sil[jj][:], acc[jj * 2][:], AF.Silu, [('acc', jj * 2)], [('sil', jj)])
                    TT('pool', hact[:, j, :], sil[jj][:], acc[jj * 2 + 1][:], ALU.mult, [('sil', jj), ('acc', jj * 2 + 1)], [('hact', j)])
            if not last:
                load_wu(0)
                load_wu(1)
            har = [('hact', j) for j in range(22)]
            for q in range(4):
                w = wd[q]
                wk = ('wd', q)
                for b in range(4):
                    pp = pd[b % 2]
                    ppk = ('pd', b % 2)
                    for j in range(22):
                        MM(pp[:, 0:256], hact[:, j, b * 128:(b + 1) * 128], w[:, j, :], j == 0, j == 21, [har[j], wk], [ppk])
                    TT('dve', h[:, b, q * 256:(q + 1) * 256], pp[:, 0:256], h[:, b, q * 256:(q + 1) * 256], ALU.add, [ppk, hk], [hk])

        def ffn_bufs(st):
            wu = [sbt(st, "wu%d" % i, [128, 8, 512], BF16) for i in range(3)]
            wd = [sbt(st, "wd%d" % i, [128, 22, 256], BF16) for i in range(4)]
            acc = [sbt(st, "acc%d" % i, [128, 512], F32) for i in range(4)]
            hact = sbt(st, "hact", [128, 22, 512], BF16)
            sil = [sbt(st, "sil%d" % i, [128, 512], F32) for i in range(2)]
            pu = [pst(st, "pu%d" % i, [128, 512], F32) for i in range(4)]
            pd = [pst(st, "pd%d" % i, [128, 512], F32) for i in range(2)]
            return wu, wd, acc, hact, sil, pu, pd

        halo = sbt(top, "halo", [128, 2, 2, 44, 2], F32)
        corr = sbt(top, "corr", [128, 44, 2], F32)
        ctm = sbt(top, "ctm", [128, 44], F32)
        MS('dve', halo[:], 0.0, [('halo', L_, p_, o_) for L_ in range(2) for p_ in range(2) for o_ in range(44)])

        with contextlib.ExitStack() as st:
            htb = [sbt(st, "ht%d" % i, [128, 4, 1024], F32) for i in range(3)]
            attb = [sbt(st, "att%d" % i, [128, 4, 512], BF16) for i in range(2)]
            xsb = [sbt(st, "xs%d" % i, [128, 4, 1024], BF16) for i in range(2)]
            xnTb = [sbt(st, "xnT%d" % i, [128, 8, 512], BF16) for i in range(2)]
            ssqb = [sbt(st, "ssq%d" % i, [128, 4], F32) for i in range(2)]
            rstdb = [sbt(st, "rstd%d" % i, [128, 4], F32) for i in range(2)]
            w_o = sbt(st, "w_o", [128, 4, 1024], BF16)
            tpb = [pst(st, "tpa%d" % i, [128, 1024], BF16) for i in range(2)]
            fb = ffn_bufs(st)
            pd_ = fb[6]
            DMA(w_o[:], WB["l0_w_out"][512:1024, :].rearrange("(k p) c -> p k c", p=128), [('wb', "l0_w_out")], ['w_o'], 'w_o')

            def load_b1(i):
                tsl = slice(i * 512, (i + 1) * 512)
                DMA(htb[i % 3][:], hp_scr[tsl, :].rearrange("(b p) d -> p b d", p=128), [('hp', i)], [('ht', i % 3)], ('ht', i % 3))
                DMA(attb[i % 2][:], att_scr[:, tsl].rearrange("(c p) t -> p c t", p=128), [('att', i)], [('attb', i % 2)], ('attb', i % 2))

            def pre_b1(i):
                tsl = slice(i * 512, (i + 1) * 512)
                ht = htb[i % 3]
                hk = ('ht', i % 3)
                att = attb[i % 2]
                ak = ('attb', i % 2)

                def wo(n_):
                    b, hf = n_ // 2, n_ % 2
                    po = pd_[n_ % 2]
                    pok = ('pd', n_ % 2)
                    for c in range(4):
                        MM(po[:], att[:, c, b * 128:(b + 1) * 128], w_o[:, c, hf * 512:(hf + 1) * 512], c == 0, c == 3, [ak, 'w_o'], [pok])
                    TT('dve', ht[:, b, hf * 512:(hf + 1) * 512], po[:], ht[:, b, hf * 512:(hf + 1) * 512], ALU.add, [pok, hk], [hk])
                    if dbg and n_ == 7:
                        DMA(hp_scr[tsl, :].rearrange("(b p) d -> p b d", p=128), ht[:], [hk], [('hpd', i)], 'hst_d')
                ns = norm_stages(ht, hk, "l0_ffn_norm", xsb[i % 2], xnTb[i % 2], ('xnT', i % 2), tpb, ssqb[i % 2], rstdb[i % 2], par=i % 2)
                return [(lambda n_=n_: wo(n_)) for n_ in range(8)] + ns

            load_b1(0)
            if NT > 1:
                load_b1(1)
            for f in pre_b1(0):
                f()
            for i in range(NT):
                if i + 2 < NT:
                    load_b1(i + 2)
                tsl = slice(i * 512, (i + 1) * 512)
                ht = htb[i % 3]
                hk = ('ht', i % 3)
                hook = None
                if i + 1 < NT:
                    stg = pre_b1(i + 1)
                    hook = (lambda k, stg=stg: stg[k - 3]() if 3 <= k < 3 + len(stg) else None)
                ffn(0, xnTb[i % 2], ht, hk, fb, i, xnkey=('xnT', i % 2), hook=hook, last=(i == NT - 1))
                DMA(h1_scr[tsl, :].rearrange("(b p) d -> p b d", p=128), ht[:], [hk], [('h1', i)], ('hst', i % 3))
        P.barrier()

        TWO_PI = 2.0 * math.pi
        with contextlib.ExitStack() as st:
            htb = [sbt(st, "ht%d" % i, [128, 4, 1024], F32) for i in range(2)]
            xs = sbt(st, "xs", [128, 4, 1024], BF16)
            xnT = sbt(st, "xnT", [128, 8, 512], BF16)
            ssq = sbt(st, "ssq", [128, 4], F32)
            rstd = sbt(st, "rstd", [128, 4], F32)
            ns512 = sbt(st, "ns512", [128, 16], F32)
            w_in1 = sbt(st, "w_in1", [128, 8, 512], BF16)
            w_glu = sbt(st, "w_glu", [128, 4, 2048], BF16)
            bglu_f = sbt(st, "bglu_f", [1, 2048], F32)
            bglu = sbt(st, "bglu", [1, 2048], BF16)
            ones_r = sbt(st, "ones_r", [1, 128], BF16)
            BbT = sbt(st, "BbT", [128, 32, 128], BF16)
            CTt = sbt(st, "CTt", [128, 48, 128], BF16)
            cosT = sbt(st, "cosT", [128, 16, 512], BF16)
            sinT = sbt(st, "sinT", [128, 16, 512], BF16)
            c512 = sbt(st, "c512", [128, 16], F32)
            s512 = sbt(st, "s512", [128, 16], F32)
            rmag = sbt(st, "rmag", [128, 16], F32)
            car = sbt(st, "car", [128, 2, 16], F32)
            ctmp = sbt(st, "ctmp", [128, 4], F32)
            tpb = [pst(st, "tpa%d" % i, [128, 1024], BF16) for i in range(2)]
            pbr = [pst(st, "pbr%d" % i, [128, 512], F32) for i in range(2)]
            pbi = [pst(st, "pbi%d" % i, [128, 512], F32) for i in range(2)]
            py = [pst(st, "py%d" % i, [128, 512], F32) for i in range(2)]
            pz = py[1]

            with contextlib.ExitStack() as s2:
                ldb = sbt(s2, "ldb", [128, 32], F32)
                dtc = sbt(s2, "dtc", [128, 16], F32)
                pr_ = [sbt(s2, "pr%d" % i, [128, 16], F32) for i in range(12)]
                Bs = [sbt(s2, "Bs%d" % i, [128, 16, 16], F32) for i in range(2)]
                Bb = [sbt(s2, "Bb%d" % i, [128, 16, 16], F32) for i in range(2)]
                Bt = sbt(s2, "Bt", [128, 16, 16], F32)
                Bpad = sbt(s2, "Bpad", [128, 128], BF16)
                Xc = [sbt(s2, "Xc%d" % i, [128, 4, 128], F32) for i in range(2)]
                Xcb = sbt(s2, "Xcb", [128, 128], BF16)
                XcT = sbt(s2, "XcT", [128, 128], BF16)
                io_i = sbt(s2, "io_i", [128, 513], I32)
                io_f = sbt(s2, "io_f", [128, 513], F32)
                ph = sbt(s2, "ph", [128, 513], F32)
                ph2 = sbt(s2, "ph2", [128, 513], F32)
                ph_i = sbt(s2, "ph_i", [128, 513], I32)
                tbl = sbt(s2, "tbl", [128, 513], F32)

                DMA(ldb[:], I["l1_log_dt"].partition_broadcast(128), (), ['ldb'], 'ldb')
                ldv = ldb[:].rearrange("p (j g) -> p j g", g=2)
                CP('dve', dtc[0:64, :], ldv[0:64, :, 0], ['ldb'], ['dtc'])
                CP('dve', dtc[64:128, :], ldv[64:128, :, 1], ['ldb'], ['dtc'])
                ACT(dtc[:], dtc[:], AF.Exp, ['dtc'], ['dtc'])
                a_re = col("l1_a_re", 0, 16)
                a_im = col("l1_a_im", 0, 16)
                ardt, fturn, mag, sinv, cosv, lbre, lbim, den, fre, fim, t0, t1 = pr_
                K = ['prm']
                TT('dve', ardt[:], a_re, dtc[:], ALU.mult, ['colv', 'dtc'], K)
                ACT(mag[:], ardt[:], AF.Exp, K, K)
                CP('dve', rmag[:], mag[:], K, ['rmag'])
                TT('dve', fturn[:], a_im, dtc[:], ALU.mult, ['colv', 'dtc'], K)
                TS('dve', fturn[:], fturn[:], 1.0 / TWO_PI, None, ALU.mult, None, K, K)

                def sincos_turns(src, n, dst_sin, dst_cos, keys):
                    for shift, dst in ((0.0, dst_sin), (0.25, dst_cos)):
                        TS('dve', ph2[:, 0:n], src, shift, None, ALU.add, None, keys, ['ph2'])
                        CP('dve', ph_i[:, 0:n], ph2[:, 0:n], ['ph2'], ['ph_i'])
                        CP('dve', tbl[:, 0:n], ph_i[:, 0:n], ['ph_i'], ['tbl'])
                        TT('dve', ph2[:, 0:n], ph2[:, 0:n], tbl[:, 0:n], ALU.subtract, ['ph2', 'tbl'], ['ph2'])
                        TS('dve', ph2[:, 0:n], ph2[:, 0:n], 0.49999, -0.49999, ALU.min, ALU.max, ['ph2'], ['ph2'])
                        ACT(dst, ph2[:, 0:n], AF.Sin, ['ph2'], keys, scale=TWO_PI)
                sincos_turns(fturn[:], 16, sinv[:], cosv[:], K)
                TT('dve', lbre[:], mag[:], cosv[:], ALU.mult, K, K)
                TT('dve', lbim[:], mag[:], sinv[:], ALU.mult, K, K)
                TT('dve', den[:], a_re, a_re, ALU.mult, ['colv'], K)
                TT('dve', t0[:], a_im, a_im, ALU.mult, ['colv'], K)
                TT('dve', den[:], den[:], t0[:], ALU.add, K, K)
                RCP(den[:], den[:], K, K)
                TS('dve', lbre[:], lbre[:], -1.0, None, ALU.add, None, K, K)
                TT('dve', t0[:], lbre[:], a_re, ALU.mult, K + ['colv'], K)
                TT('dve', t1[:], lbim[:], a_im, ALU.mult, K + ['colv'], K)
                TT('dve', fre[:], t0[:], t1[:], ALU.add, K, K)
                TT('dve', fre[:], fre[:], den[:], ALU.mult, K, K)
                TT('dve', t0[:], lbim[:], a_re, ALU.mult, K + ['colv'], K)
                TT('dve', t1[:], lbre[:], a_im, ALU.mult, K + ['colv'], K)
                TT('dve', fim[:], t0[:], t1[:], ALU.subtract, K, K)
                TT('dve', fim[:], fim[:], den[:], ALU.mult, K, K)
                DMA(Bs[0][:], I["l1_b_re"].rearrange("g p c -> (g p) c").rearrange("(j s) c -> s j c", s=128), (), ['Bs0'], 'Bs0')
                DMA(Bs[1][:], I["l1_b_im"].rearrange("g p c -> (g p) c").rearrange("(j s) c -> s j c", s=128), (), ['Bs1'], 'Bs1')
                freb = fre[:].unsqueeze(2).to_broadcast([128, 16, 16])
                fimb = fim[:].unsqueeze(2).to_broadcast([128, 16, 16])
                TT('dve', Bb[0][:], Bs[0][:], freb, ALU.mult, ['Bs0'] + K, ['Bb0'])
                TT('dve', Bt[:], Bs[1][:], fimb, ALU.mult, ['Bs1'] + K, ['Bt'])
                TT('dve', Bb[0][:], Bb[0][:], Bt[:], ALU.subtract, ['Bb0', 'Bt'], ['Bb0'])
                TT('dve', Bb[1][:], Bs[1][:], freb, ALU.mult, ['Bs1'] + K, ['Bb1'])
                TT('dve', Bt[:], Bs[0][:], fimb, ALU.mult, ['Bs0'] + K, ['Bt'])
                TT('dve', Bb[1][:], Bb[1][:], Bt[:], ALU.add, ['Bb1', 'Bt'], ['Bb1'])
                for j in range(16):
                    for ri in range(2):
                        MS('pool', Bpad[:], 0.0, ['Bpad'])
                        base = (j % 4) * 32
                        CP('dve', Bpad[0:64, base:base + 16], Bb[ri][0:64, j, :], ['Bb%d' % ri, 'Bpad'], ['Bpad'])
                        CP('dve', Bpad[64:128, base + 16:base + 32], Bb[ri][64:128, j, :], ['Bb%d' % ri, 'Bpad'], ['Bpad'])
                        TR(tpb[0][:, 0:128], Bpad[:], ident[:], ['Bpad', 'ident'], [('tp', 0)])
                        CP('dve', BbT[:, j * 2 + ri, :], tpb[0][:, 0:128], [('tp', 0)], ['BbT'])
                for ri, nm in enumerate(["l1_c_re", "l1_c_im"]):
                    MS('pool', Xc[ri][:], 0.0, ['Xc%d' % ri])
                    for g in range(32):
                        ct, g8 = g // 8, g % 8
                        gl = g8 % 2
                        DMA(Xc[ri][g8 * 16:(g8 + 1) * 16, ct, gl * 64:(gl + 1) * 64], I[nm][g], ['Xc%d' % ri], [('Xcg', ri, g)], 'Xc%d' % ri)
                MS('pool', CTt[:], 0.0, ['CTt'])
                for var_i, (ri, sgn) in enumerate(((0, 1.0), (1, -1.0), (0, -1.0))):
                    for ct in range(4):
                        TS('dve', Xcb[:], Xc[ri][:, ct, :], sgn, None, ALU.mult, None,
                           ['Xc%d' % ri] + [('Xcg', ri, g) for g in range(32)], ['Xcb'])
                        TR(tpb[1][:, 0:128], Xcb[:], ident[:], ['Xcb', 'ident'], [('tp', 1)])
                        for jj in range(4):
                            j = ct * 4 + jj
                            CP('dve', CTt[:, j * 3 + var_i, jj * 32:(jj + 1) * 32], tpb[1][:, jj * 32:(jj + 1) * 32], [('tp', 1)], ['CTt'])
                P.op('pool', lambda e: e.iota(io_i[:], [[1, 513]], base=0, channel_multiplier=0), (), ['io_i'])
                CP('dve', io_f[:], io_i[:], ['io_i'], ['io_f'])
                for j in range(16):
                    TS('dve', ph[:], io_f[:], fturn[:, j:j + 1], None, ALU.mult, None, ['io_f'] + K, ['ph'])
                    for shift, dstT, dst512 in ((0.0, sinT, s512), (0.25, cosT, c512)):
                        TS('dve', ph2[:], ph[:], shift, None, ALU.add, None, ['ph'], ['ph2'])
                        CP('dve', ph_i[:], ph2[:], ['ph2'], ['ph_i'])
                        CP('dve', tbl[:], ph_i[:], ['ph_i'], ['tbl'])
                        TT('dve', ph2[:], ph2[:], tbl[:], ALU.subtract, ['ph2', 'tbl'], ['ph2'])
                        TS('dve', ph2[:], ph2[:], 0.49999, -0.49999, ALU.min, ALU.max, ['ph2'], ['ph2'])
                        ACT(tbl[:], ph2[:], AF.Sin, ['ph2'], ['tbl'], scale=TWO_PI)
                        CP('dve', dstT[:, j, :], tbl[:, 0:512], ['tbl'], ['tabs'])
                        CP('dve', dst512[:, j:j + 1], tbl[:, 512:513], ['tbl'], ['tabs'])
            P.barrier()
            uT = sbt(st, "uT", [128, 4, 512], BF16)
            zt = [sbt(st, "zt%d" % i, [128, 512], F32) for i in range(4)]
            ztb = [sbt(st, "ztb%d" % i, [128, 512], BF16) for i in range(8)]
            nident = sbt(st, "nident", [128, 128], BF16)
            wre = [sbt(st, "wre%d" % i, [128, 512], F32) for i in range(2)]
            wim = [sbt(st, "wim%d" % i, [128, 512], F32) for i in range(2)]
            ot_ = [sbt(st, "ot%d" % i, [128, 512], BF16) for i in range(8)]
            yv, g1, g2, sg = zt
            yT = sbt(st, "yT", [128, 4, 512], BF16)
            DMA(w_in1[:], WB["l1_w_in"].rearrange("(k p) c -> p k c", p=128), [('wb', "l1_w_in")], ['w_in1'], 'w_in1')
            DMA(w_glu[:], WB["l1_w_glu"].rearrange("(k p) c -> p k c", p=128), [('wb', "l1_w_glu")], ['w_glu'], 'w_glu')
            DMA(bglu_f[:], I["l1_b_glu"].rearrange("(o c) -> o c", o=1), (), ['bglu_f'], 'bglu_f')
            CP('dve', bglu[:], bglu_f[:], ['bglu_f'], ['bglu'])
            MS('pool', ones_r[:], 1.0, ['ones_r'])
            MS('pool', car[:], 0.0, [('car', j_) for j_ in range(16)])
            TS('dve', nident[:], ident[:], -1.0, None, ALU.mult, None, ['ident'], ['nident'])
            TS('dve', ns512[:], s512[:], -1.0, None, ALU.mult, None, ['tabs'], ['tabs'])

            def load_b2(i):
                DMA(htb[i % 2][:], h1_scr[i * 512:(i + 1) * 512, :].rearrange("(b p) d -> p b d", p=128), [('h1', i)], [('ht', i % 2)], ('ht', i % 2))
            load_b2(0)
            for i in range(NT):
                if i + 1 < NT:
                    load_b2(i + 1)
                tsl = slice(i * 512, (i + 1) * 512)
                ht = htb[i % 2]
                hk = ('ht', i % 2)
                norm_T("b2", ht, hk, "l1_mix_norm", xs, xnT, 'xnT', tpb, ssq, rstd)
                xr = [('xnT', kc) for kc in range(8)]
                def emit_u(c, bank):
                    pzb = py[bank]
                    pzk = ('py', bank)
                    for kc in range(8):
                        MM(pzb[:], w_in1[:, kc, c * 128:(c + 1) * 128], xnT[:, kc, :], kc == 0, kc == 7, ['w_in1', xr[kc]], [pzk])
                    ACT(uT[:, c, :], pzb[:], AF.Copy, [pzk], [('uT', c)])
                emit_u(0, 1)
                def emit_b(j):
                    c = j // 4
                    s = j % 2
                    uk = ('uT', c)
                    MM(pbr[s][:], BbT[:, j * 2, :], uT[:, c, :], True, True, ['BbT', uk], [('pbr', s)])
                    MM(pbi[s][:], BbT[:, j * 2 + 1, :], uT[:, c, :], True, True, ['BbT', uk], [('pbi', s)])

                def emit_zscan(j):
                    s = j % 2
                    cs = cosT[:, j, :]
                    sn = sinT[:, j, :]
                    z4 = [ztb[s * 4 + k] for k in range(4)]
                    zk4 = [('ztb', s * 4 + k) for k in range(4)]
                    TT('dve', z4[0][:], pbr[s][:], cs, ALU.mult, [('pbr', s), 'tabs'], [zk4[0]])
                    TT('dve', z4[1][:], pbi[s][:], sn, ALU.mult, [('pbi', s), 'tabs'], [zk4[1]])
                    TT('dve', z4[2][:], pbi[s][:], cs, ALU.mult, [('pbi', s), 'tabs'], [zk4[2]])
                    TT('dve', z4[3][:], pbr[s][:], sn, ALU.mult, [('pbr', s), 'tabs'], [zk4[3]])
                    zre = tpb[0][:].bitcast(F32)
                    zim = tpb[1][:].bitcast(F32)
                    MM(zre, ident[:], z4[0][:], True, False, ['ident', zk4[0]], [('tp', 0)])
                    MM(zre, ident[:], z4[1][:], False, True, ['ident', zk4[1]], [('tp', 0)])
                    MM(zim, ident[:], z4[2][:], True, False, ['ident', zk4[2]], [('tp', 1)])
                    MM(zim, nident[:], z4[3][:], False, True, ['nident', zk4[3]], [('tp', 1)])
                    rb = rmag[:, j:j + 1].to_broadcast([128, 512])
                    SCAN(wre[s][:], rb, zre, car[:, 0, j:j + 1], ['rmag', ('tp', 0), ('car', j)], [('wre', s)])
                    SCAN(wim[s][:], rb, zim, car[:, 1, j:j + 1], ['rmag', ('tp', 1), ('car', j)], [('wim', s)])
                    ACT(ctmp[:, 0:1], wim[s][:, 511:512], AF.Identity, [('wim', s), 'tabs'], ['ctmp0'], scale=ns512[:, j:j + 1])
                    ACT(car[:, 0, j:j + 1], wre[s][:, 511:512], AF.Identity, [('wre', s), 'tabs', 'ctmp0'], [('car', j)],
                        scale=c512[:, j:j + 1], bias=ctmp[:, 0:1])
                    ACT(ctmp[:, 1:2], wim[s][:, 511:512], AF.Identity, [('wim', s), 'tabs'], ['ctmp1'], scale=c512[:, j:j + 1])
                    ACT(car[:, 1, j:j + 1], wre[s][:, 511:512], AF.Identity, [('wre', s), 'tabs', 'ctmp1'], [('car', j)],
                        scale=s512[:, j:j + 1], bias=ctmp[:, 1:2])

                def emit_rot(j):
                    s = j % 2
                    cs = cosT[:, j, :]
                    sn = sinT[:, j, :]
                    o4 = [ot_[s * 4 + k] for k in range(4)]
                    ok4 = [('ot', s * 4 + k) for k in range(4)]
                    TT('pool', o4[0][:], wre[s][:], cs, ALU.mult, [('wre', s), 'tabs'], [ok4[0]])
                    TT('pool', o4[1][:], wim[s][:], sn, ALU.mult, [('wim', s), 'tabs'], [ok4[1]])
                    TT('pool', o4[2][:], wre[s][:], sn, ALU.mult, [('wre', s), 'tabs'], [ok4[2]])
                    TT('pool', o4[3][:], wim[s][:], cs, ALU.mult, [('wim', s), 'tabs'], [ok4[3]])

                def emit_y(j):
                    c = j // 4
                    s = j % 2
                    pyc = py[c % 2]
                    pyk = ('py', c % 2)
                    o4 = [ot_[s * 4 + k] for k in range(4)]
                    ok4 = [('ot', s * 4 + k) for k in range(4)]
                    MM(pyc[:], CTt[:, j * 3, :], o4[0][:], j % 4 == 0, False, ['CTt', ok4[0]], [pyk])
                    MM(pyc[:], CTt[:, j * 3 + 2, :], o4[1][:], False, False, ['CTt', ok4[1]], [pyk])
                    MM(pyc[:], CTt[:, j * 3 + 1, :], o4[2][:], False, False, ['CTt', ok4[2]], [pyk])
                    MM(pyc[:], CTt[:, j * 3 + 1, :], o4[3][:], False, j % 4 == 3, ['CTt', ok4[3]], [pyk])

                def emit_gelu(c):
                    uk = ('uT', c)
                    pyc = py[c % 2]
                    pyk = ('py', c % 2)
                    STT(yv[:], uT[:, c, :], col("l1_d", c), pyc[:], ALU.mult, ALU.add, [uk, 'colv', pyk], ['zt0'])
                    TT('dve', g1[:], yv[:], yv[:], ALU.mult, ['zt0'], ['zt1'])
                    TS('dve', g1[:], g1[:], 0.044715, 1.0, ALU.mult, ALU.add, ['zt1'], ['zt1'])
                    TT('dve', g1[:], g1[:], yv[:], ALU.mult, ['zt1', 'zt0'], ['zt1'])
                    ACT(g2[:], g1[:], AF.Sigmoid, ['zt1'], ['zt2'], scale=1.5957691216057308)
                    TT('dve', yT[:, c, :], yv[:], g2[:], ALU.mult, ['zt0', 'zt2'], [('yT', c)])

                emit_b(0)
                for j in range(16):
                    emit_zscan(j)
                    if j + 1 < 16:
                        emit_b(j + 1)
                    emit_rot(j)
                    if j >= 1:
                        emit_y(j - 1)
                        if (j - 1) % 4 == 3:
                            emit_gelu((j - 1) // 4)
                    if j % 4 == 2 and j // 4 < 3:
                        emit_u(j // 4 + 1, (j // 4 + 1) % 2)
                emit_y(15)
                emit_gelu(3)
                yr = [('yT', c) for c in range(4)]
                for b in range(4):
                    for hf in range(2):
                        pv = pbr[hf]
                        pgt = pbi[hf]
                        for (pp, ppk, cofs) in ((pv, ('pbr', hf), hf * 512), (pgt, ('pbi', hf), 1024 + hf * 512)):
                            MM(pp[:], ones_r[0:1, :], bglu[0:1, cofs:cofs + 512], True, False, ['ones_r', 'bglu'], [ppk])
                            for c in range(4):
                                MM(pp[:], yT[:, c, b * 128:(b + 1) * 128], w_glu[:, c, cofs:cofs + 512], False, c == 3, [yr[c], 'w_glu'], [ppk])
                        ACT(sg[:], pgt[:], AF.Sigmoid, [('pbi', hf)], ['zt3'])
                        TT('dve', sg[:], pv[:], sg[:], ALU.mult, [('pbr', hf), 'zt3'], ['zt3'])
                        TT('dve', ht[:, b, hf * 512:(hf + 1) * 512], sg[:], ht[:, b, hf * 512:(hf + 1) * 512], ALU.add, ['zt3', hk], [hk])
                DMA(h1m_scr[tsl, :].rearrange("(b p) d -> p b d", p=128), ht[:], [hk], [('h1m', i)], ('hst', i % 2))
        P.barrier()

        with contextlib.ExitStack() as st:
            htb = [sbt(st, "ht%d" % i, [128, 4, 1024], F32) for i in range(3)]
            xsb = [sbt(st, "xs%d" % i, [128, 4, 1024], BF16) for i in range(2)]
            xnTb = [sbt(st, "xnT%d" % i, [128, 8, 512], BF16) for i in range(2)]
            ssqb = [sbt(st, "ssq%d" % i, [128, 4], F32) for i in range(2)]
            rstdb = [sbt(st, "rstd%d" % i, [128, 4], F32) for i in range(2)]
            ssqf = sbt(st, "ssqf", [128, 4], F32)
            rstdf = sbt(st, "rstdf", [128, 4], F32)
            junk = sbt(st, "junkf", [128, 1024], BF16)
            tpb = [pst(st, "tpa%d" % i, [128, 1024], BF16) for i in range(2)]
            fb = ffn_bufs(st)

            def load_b3(i):
                DMA(htb[i % 3][:], h1m_scr[i * 512:(i + 1) * 512, :].rearrange("(b p) d -> p b d", p=128), [('h1m', i)], [('ht', i % 3)], ('ht', i % 3))

            def pre_b3(i):
                return norm_stages(htb[i % 3], ('ht', i % 3), "l1_ffn_norm", xsb[i % 2], xnTb[i % 2], ('xnT', i % 2), tpb,
                                   ssqb[i % 2], rstdb[i % 2], par=i % 2)
            load_b3(0)
            if NT > 1:
                load_b3(1)
            for f in pre_b3(0):
                f()
            for i in range(NT):
                if i + 2 < NT:
                    load_b3(i + 2)
                tsl = slice(i * 512, (i + 1) * 512)
                ht = htb[i % 3]
                hk = ('ht', i % 3)
                hook = None
                if i + 1 < NT:
                    stg = pre_b3(i + 1)
                    hook = (lambda k, stg=stg: stg[k - 8]() if 8 <= k < 8 + len(stg) else None)
                ffn(1, xnTb[i % 2], ht, hk, fb, i, xnkey=('xnT', i % 2), hook=hook, last=(i == NT - 1))
                for b in range(4):
                    ACT(junk[:], ht[:, b, :], AF.Square, [hk], ['junkf', 'ssqf'], accum=ssqf[:, b:b + 1])
                TS('dve', rstdf[:], ssqf[:], 1.0 / D, EPS, ALU.mult, ALU.add, ['ssqf'], ['rstdf'])
                ACT(rstdf[:], rstdf[:], AF.Sqrt, ['rstdf'], ['rstdf'])
                RCP(rstdf[:], rstdf[:], ['rstdf'], ['rstdf'])
                for b in range(4):
                    STT(ht[:, b, :], ht[:, b, :], rstdf[:, b:b + 1], gfin[:], ALU.mult, ALU.mult, [hk, 'rstdf', 'gfin'], [hk])
                DMA(out[tsl, :].rearrange("(b p) d -> p b d", p=128), ht[:], [hk], [('out', i)], ('hst', i % 3))
        P.barrier()
        P.emit()
    return nc


def host_consts(S):
    pos = np.arange(S, dtype=np.float32)
    inv = (10000.0 ** (-np.arange(16, dtype=np.float32) / 16.0)).astype(np.float32)
    ang = pos[:, None] * inv[None, :]
    cos, sin = np.cos(ang).astype(np.float32), np.sin(ang).astype(np.float32)
    sc = np.float32(96.0 ** -0.5)
    rq_c = np.concatenate([cos.T, cos.T], 0) * sc
    rq_s = np.concatenate([-sin.T, sin.T], 0) * sc
    return dict(rq_c=np.ascontiguousarray(rq_c, np.float32), rq_s=np.ascontiguousarray(rq_s, np.float32),
                rk_c=np.ascontiguousarray(cos), rk_s=np.ascontiguousarray(sin))


def swap_uq(w_uq):
    w = np.array(w_uq, np.float32).reshape(256, 8, 96).copy()
    sw = w.copy()
    sw[:, :, 64:80] = w[:, :, 80:96]
    sw[:, :, 80:96] = w[:, :, 64:80]
    return np.ascontiguousarray(sw.reshape(256, 768))


_NC_CACHE = {}


def kernel(**inputs):
    x = np.asarray(inputs["x"], np.float32)
    B, S, _ = x.shape
    if S not in _NC_CACHE:
        _NC_CACHE[S] = build(S)
    nc = _NC_CACHE[S]
    consts = host_consts(S)
    base = {name: np.ascontiguousarray(np.asarray(inputs[name], np.float32)) for name, _ in PARAM_SHAPES}
    base["w_uq_sw"] = swap_uq(inputs["l0_w_uq"])
    base.update(consts)
    n = 8
    in_maps = []
    for c in range(n):
        m = dict(base)
        m["x"] = np.ascontiguousarray(x[c % B])
        in_maps.append(m)
    res = run_bass_kernel_spmd(nc, in_maps, core_ids=list(range(n)))
    return np.stack([np.asarray(res.results[b]["out"], np.float32) for b in range(B)], 0)
```
